# Optimizing a Trainium2 kernel written in Bass

```python
import math
import jax, jax.numpy as jnp
from jax import lax
import numpy as np

D_MODEL = 1024
BATCH = 16
SEQ = 2048
DEPTH = 1
DEC_BATCH = 16
DEC_SEQ = 16
PAST_LEN = 2048

CHUNK = 64
M_HEADS = 4
M_HEAD_DIM = 128
M_WIDTH = M_HEADS * M_HEAD_DIM
CONV_WIDTH = 4
A_HEADS = 8
A_HEAD_DIM = 64
A_WIDTH = A_HEADS * A_HEAD_DIM
BAND_CHUNKS = 8
WINDOW = BAND_CHUNKS * CHUNK
REL_CLIP = 128
N_REL = 2 * REL_CLIP + 1
EPS = 1e-6
IN_WIDTHS = (M_WIDTH, M_WIDTH, M_WIDTH, M_WIDTH, M_WIDTH, M_HEADS, M_HEADS,
             A_WIDTH, A_WIDTH, A_WIDTH, A_WIDTH, D_MODEL, D_MODEL)
D_IN = sum(IN_WIDTHS)
F_GATE_OFFSET = sum(IN_WIDTHS[:6])

kernel_name = "hybrid_mlstm_chunkband_stream_step"


def rms_norm(x, g):
    xf = x.astype(jnp.float32)
    y = xf * lax.rsqrt(jnp.mean(xf * xf, axis=-1, keepdims=True) + EPS)
    return (y * g.astype(jnp.float32)).astype(x.dtype)


def in_projection(x, norm_g, w_in, b_in):
    h = rms_norm(x, norm_g)
    z = jnp.einsum("btd,de->bte", h, w_in) + b_in
    points = [int(p) for p in np.cumsum(IN_WIDTHS)[:-1]]
    return jnp.split(z, points, axis=-1)


def causal_conv(u, buf, w, b):
    T = u.shape[1]
    full = jnp.concatenate([buf.astype(u.dtype), u], axis=1)
    out = b + sum(full[:, j:j + T] * w[j] for j in range(CONV_WIDTH))
    return out, full[:, full.shape[1] - (CONV_WIDTH - 1):]


def mlstm_chunk(carry, inp):
    C, n, m = carry
    q, k, v, ig, fg = inp
    L = q.shape[1]
    b = jnp.cumsum(jax.nn.log_sigmoid(fg), axis=1)
    causal = jnp.tril(jnp.ones((L, L), dtype=bool))[None, :, :, None]
    logw = jnp.where(causal, b[:, :, None, :] - b[:, None, :, :] + ig[:, None, :, :], -jnp.inf)
    log_state = b + m[:, None, :]
    m_t = jnp.maximum(log_state, jnp.max(logw, axis=2))
    w_intra = jnp.exp(logw - m_t[:, :, None, :])
    w_state = jnp.exp(log_state - m_t)
    s = jnp.einsum("bthd,bshd->btsh", q, k) * w_intra
    num = jnp.einsum("btsh,bshd->bthd", s, v) + w_state[..., None] * jnp.einsum("bthk,bhkv->bthv", q, C)
    den = jnp.sum(s, axis=2) + w_state * jnp.einsum("bthk,bhk->bth", q, n)
    h = num / jnp.maximum(jnp.abs(den), jnp.exp(-m_t))[..., None]
    m_new = m_t[:, -1]
    w_end = jnp.exp(b[:, -1:] - b + ig - m_new[:, None])
    decay = jnp.exp(b[:, -1] + m - m_new)
    C_new = decay[..., None, None] * C + jnp.einsum("bsh,bshk,bshv->bhkv", w_end, k, v)
    n_new = decay[..., None] * n + jnp.einsum("bsh,bshk->bhk", w_end, k)
    return (C_new, n_new, m_new), h


def mlstm_branch(mq, mk, mv, mo, mi, mf, conv_buf, C0, n0, m0, conv_w, conv_b, m_head_g):
    B, T, _ = mq.shape
    f32 = jnp.float32
    qk, conv_new = causal_conv(jnp.concatenate([mq, mk], axis=-1), conv_buf, conv_w, conv_b)
    q, k = jnp.split(jax.nn.silu(qk), 2, axis=-1)
    q = q.reshape(B, T, M_HEADS, M_HEAD_DIM).astype(f32)
    k = k.reshape(B, T, M_HEADS, M_HEAD_DIM).astype(f32) * (M_HEAD_DIM ** -0.5)
    v = mv.reshape(B, T, M_HEADS, M_HEAD_DIM).astype(f32)
    ig = mi.astype(f32)
    fg = mf.astype(f32)
    nc = max(T // CHUNK, 1)
    L = T // nc

    def to_blocks(a):
        return jnp.moveaxis(a.reshape(B, nc, L, *a.shape[2:]), 1, 0)

    (C1, n1, m1), h = lax.scan(mlstm_chunk, (C0.astype(f32), n0.astype(f32), m0.astype(f32)),
                               (to_blocks(q), to_blocks(k), to_blocks(v), to_blocks(ig), to_blocks(fg)))
    h = jnp.moveaxis(h, 0, 1).reshape(B, T, M_HEADS, M_HEAD_DIM)
    h = rms_norm(h, m_head_g).reshape(B, T, M_WIDTH)
    h = (jax.nn.sigmoid(mo.astype(f32)) * h).astype(mq.dtype)
    return h, conv_new, C1.astype(C0.dtype), n1.astype(n0.dtype), m1.astype(m0.dtype)


def attn_heads(aq, ak, av, q_g, k_g):
    B, T, _ = aq.shape
    q = rms_norm(aq.reshape(B, T, A_HEADS, A_HEAD_DIM), q_g)
    k = rms_norm(ak.reshape(B, T, A_HEADS, A_HEAD_DIM), k_g)
    v = av.reshape(B, T, A_HEADS, A_HEAD_DIM)
    return q, k, v


def attend(q, k, v, dist, valid, rel_bias):
    bias = rel_bias[:, jnp.clip(dist, -REL_CLIP, REL_CLIP) + REL_CLIP].astype(jnp.float32)
    s = jnp.einsum("bqhd,bkhd->bhqk", q, k, preferred_element_type=jnp.float32) * (A_HEAD_DIM ** -0.5) + bias
    if valid is not None:
        s = jnp.where(valid, s, -jnp.inf)
    p = jax.nn.softmax(s, axis=-1)
    return jnp.einsum("bhqk,bkhd->bqhd", p.astype(v.dtype), v)


def band_attention_prompt(q, k, v, rel_bias):
    B, T, H, D = q.shape
    nc = T // CHUNK
    band = WINDOW + CHUNK
    kp = jnp.pad(k, ((0, 0), (WINDOW, 0), (0, 0), (0, 0)))
    vp = jnp.pad(v, ((0, 0), (WINDOW, 0), (0, 0), (0, 0)))
    qc = q.reshape(B, nc, CHUNK, H, D)
    key_idx = jnp.arange(band)
    dist = jnp.arange(CHUNK)[:, None] + WINDOW - key_idx[None, :]

    def one_chunk(c):
        kb = lax.dynamic_slice_in_dim(kp, c * CHUNK, band, axis=1)
        vb = lax.dynamic_slice_in_dim(vp, c * CHUNK, band, axis=1)
        qb = lax.dynamic_index_in_dim(qc, c, axis=1, keepdims=False)
        valid = (key_idx >= WINDOW - c * CHUNK)[None, :]
        return attend(qb, kb, vb, dist, valid, rel_bias)

    out = lax.map(one_chunk, jnp.arange(nc))
    return jnp.moveaxis(out, 0, 1).reshape(B, T, H * D)


def band_attention_cached(q, k, v, k_past, v_past, rel_bias):
    B, T, H, D = q.shape
    Lc = k_past.shape[1]
    kb = jnp.concatenate([k_past.astype(k.dtype), k], axis=1)
    vb = jnp.concatenate([v_past.astype(v.dtype), v], axis=1)
    dist = jnp.arange(T)[:, None] + Lc - jnp.arange(Lc + T)[None, :]
    return attend(q, kb, vb, dist, None, rel_bias).reshape(B, T, H * D)


def merge_out(x, h_m, mz, h_a, az, gm, ga, w_bm, w_ba, w_out):
    u_m = jnp.einsum("btc,cd->btd", h_m * jax.nn.silu(mz), w_bm)
    u_a = jnp.einsum("btc,cd->btd", h_a * jax.nn.silu(az), w_ba)
    mix = jax.nn.sigmoid(gm) * u_m + jax.nn.sigmoid(ga) * u_a
    return x + jnp.einsum("btd,de->bte", mix, w_out)


def setup_inputs(seed: int = 0) -> dict:
    key = jax.random.key(seed)
    ks = jax.random.split(key, 20)
    f32 = jnp.float32

    def nrm(k, shape, scale):
        return scale * jax.random.normal(k, shape, f32)

    lc = min(WINDOW, PAST_LEN)
    b_in = nrm(ks[10], (DEPTH, D_IN), 0.01)
    b_in = b_in.at[:, F_GATE_OFFSET:F_GATE_OFFSET + M_HEADS].add(jnp.linspace(3.0, 6.0, M_HEADS, dtype=f32))
    return {
        "x_prompt": nrm(ks[0], (BATCH, SEQ, D_MODEL), 1.0),
        "x_sample": nrm(ks[1], (DEC_BATCH, DEC_SEQ, D_MODEL), 1.0),
        "state_mlstm_C": nrm(ks[2], (DEPTH, DEC_BATCH, M_HEADS, M_HEAD_DIM, M_HEAD_DIM), 0.1),
        "state_mlstm_n": nrm(ks[3], (DEPTH, DEC_BATCH, M_HEADS, M_HEAD_DIM), 0.1),
        "state_mlstm_m": nrm(ks[4], (DEPTH, DEC_BATCH, M_HEADS), 1.0),
        "state_mlstm_conv": nrm(ks[5], (DEPTH, DEC_BATCH, CONV_WIDTH - 1, 2 * M_WIDTH), 1.0),
        "cache_attn_k": nrm(ks[6], (DEPTH, DEC_BATCH, lc, A_HEADS, A_HEAD_DIM), 1.0),
        "cache_attn_v": nrm(ks[7], (DEPTH, DEC_BATCH, lc, A_HEADS, A_HEAD_DIM), 1.0),
        "norm_g": 1.0 + nrm(ks[8], (DEPTH, D_MODEL), 0.02),
        "w_in": nrm(ks[9], (DEPTH, D_MODEL, D_IN), D_MODEL ** -0.5),
        "b_in": b_in,
        "conv_w": nrm(ks[11], (DEPTH, CONV_WIDTH, 2 * M_WIDTH), CONV_WIDTH ** -0.5),
        "conv_b": nrm(ks[12], (DEPTH, 2 * M_WIDTH), 0.01),
        "m_head_g": 1.0 + nrm(ks[13], (DEPTH, M_HEADS, M_HEAD_DIM), 0.02),
        "q_norm_g": 1.0 + nrm(ks[14], (DEPTH, A_HEAD_DIM), 0.02),
        "k_norm_g": 1.0 + nrm(ks[15], (DEPTH, A_HEAD_DIM), 0.02),
        "rel_bias": nrm(ks[16], (DEPTH, A_HEADS, N_REL), 0.1),
        "w_bm": nrm(ks[17], (DEPTH, M_WIDTH, D_MODEL), M_WIDTH ** -0.5),
        "w_ba": nrm(ks[18], (DEPTH, A_WIDTH, D_MODEL), A_WIDTH ** -0.5),
        "w_out": nrm(ks[19], (DEPTH, D_MODEL, D_MODEL), D_MODEL ** -0.5),
    }


def reference(x_prompt, x_sample, state_mlstm_C, state_mlstm_n, state_mlstm_m, state_mlstm_conv,
              cache_attn_k, cache_attn_v, norm_g, w_in, b_in, conv_w, conv_b, m_head_g,
              q_norm_g, k_norm_g, rel_bias, w_bm, w_ba, w_out):
    xp, xs = x_prompt, x_sample
    Bp, Tp, _ = xp.shape
    keep = min(WINDOW, Tp)
    pC, pn, pm, pconv, pk, pv = [], [], [], [], [], []
    sC, sn, sm, sconv, sk, sv = [], [], [], [], [], []
    for l in range(DEPTH):
        mq, mk, mv, mo, mz, mi, mf, aq, ak, av, az, gm, ga = in_projection(xp, norm_g[l], w_in[l], b_in[l])
        conv0 = jnp.zeros((Bp, CONV_WIDTH - 1, 2 * M_WIDTH), xp.dtype)
        C0 = jnp.zeros((Bp, M_HEADS, M_HEAD_DIM, M_HEAD_DIM), jnp.float32)
        n0 = jnp.zeros((Bp, M_HEADS, M_HEAD_DIM), jnp.float32)
        m0 = jnp.zeros((Bp, M_HEADS), jnp.float32)
        h_m, conv_p, C_p, n_p, m_p = mlstm_branch(mq, mk, mv, mo, mi, mf, conv0, C0, n0, m0,
                                                  conv_w[l], conv_b[l], m_head_g[l])
        q, k, v = attn_heads(aq, ak, av, q_norm_g[l], k_norm_g[l])
        h_a = band_attention_prompt(q, k, v, rel_bias[l])
        xp = merge_out(xp, h_m, mz, h_a, az, gm, ga, w_bm[l], w_ba[l], w_out[l])
        pC.append(C_p); pn.append(n_p); pm.append(m_p); pconv.append(conv_p)
        pk.append(k[:, Tp - keep:]); pv.append(v[:, Tp - keep:])

        mq, mk, mv, mo, mz, mi, mf, aq, ak, av, az, gm, ga = in_projection(xs, norm_g[l], w_in[l], b_in[l])
        h_m, conv_s, C_s, n_s, m_s = mlstm_branch(mq, mk, mv, mo, mi, mf, state_mlstm_conv[l],
                                                  state_mlstm_C[l], state_mlstm_n[l], state_mlstm_m[l],
                                                  conv_w[l], conv_b[l], m_head_g[l])
        q, k, v = attn_heads(aq, ak, av, q_norm_g[l], k_norm_g[l])
        h_a = band_attention_cached(q, k, v, cache_attn_k[l], cache_attn_v[l], rel_bias[l])
        xs = merge_out(xs, h_m, mz, h_a, az, gm, ga, w_bm[l], w_ba[l], w_out[l])
        sC.append(C_s); sn.append(n_s); sm.append(m_s); sconv.append(conv_s)
        sk.append(k); sv.append(v)

    return (xp, xs,
            jnp.stack(pC), jnp.stack(pn), jnp.stack(pm), jnp.stack(pconv), jnp.stack(pk), jnp.stack(pv),
            jnp.stack(sC), jnp.stack(sn), jnp.stack(sm), jnp.stack(sconv), jnp.stack(sk), jnp.stack(sv))
```

```python
import contextlib
import math
import numpy as np
import concourse.bass as bass
import concourse.mybir as mybir
from concourse.bass_utils import run_bass_kernel_spmd

F32 = mybir.dt.float32
BF16 = mybir.dt.bfloat16
AF = mybir.ActivationFunctionType
ALU = mybir.AluOpType

NCORES = 8
D = 1024
SEQ = 2048
TS = 16
EPS = 1e-6
EPOCH = 12000
NDMASEM = 3
LNCK = math.log(128.0 ** -0.5)

COLS = dict(mq=0, mk=512, mv=1024, mo=1536, mz=2048, mi=2560, mf=2564, aq=2568, ak=3080,
            av=3592, az=4104, gm=4616, ga=5640)
STREAM = [("mq", 0), ("mk", 512), ("mv", 1024), ("mo", 1536), ("mz", 2048),
          ("aq", 2568), ("ak", 3080), ("av", 3592), ("az", 4104),
          ("gm0", 4616), ("gm1", 5128), ("ga0", 5640), ("ga1", 6152)]
FM_CH = dict(mq=0, mk=4, aq=8, ak=12, gm0=16, gm1=20, ga0=24, ga1=28)
TM_IDX = dict(mv=0, mo=1, mz=2, av=3, az=4)
NWB = 3
PSUM_KEYS = frozenset(['pA', 'pB', 'pT', 'pS', 'pS2', 'pX0', 'pX1', 'pG'])
import os as _os
PROBE_NOLOAD = _os.environ.get('KPROBE', '') == 'noload'


class Sched:
    ENGS = ("pe", "act", "dve", "pool", "sp")

    def __init__(self, nc):
        self.nc = nc
        self.ops = []
        self.per_eng = {e: [] for e in self.ENGS}
        self.last_w = {}
        self.readers = {}
        self.dma_count = {e: 0 for e in self.ENGS}

    dry = False

    def op(self, eng, fn, reads=(), writes=(), dma=False):
        if self.dry:
            return -1
        oid = len(self.ops)
        deps = set()
        for k in reads:
            w = self.last_w.get(k)
            if w is not None:
                deps.add(w)
            if k in PSUM_KEYS:
                for r in self.readers.get(k, ()):
                    if self.ops[r]["eng"] != eng:
                        deps.add(r)
        for k in writes:
            w = self.last_w.get(k)
            if w is not None:
                deps.add(w)
            for r in self.readers.get(k, ()):
                deps.add(r)
        rec = dict(eng=eng, fn=fn, deps=deps, dma=dma, idx=len(self.per_eng[eng]), id=oid)
        if dma:
            rec["dma_i"] = self.dma_count[eng]
            self.dma_count[eng] += 1
        self.ops.append(rec)
        self.per_eng[eng].append(rec)
        for k in reads:
            self.readers.setdefault(k, []).append(oid)
        for k in writes:
            self.last_w[k] = oid
            self.readers[k] = []
        return oid

    def emit(self, final_wait_eng="sp"):
        nc = self.nc
        ops = self.ops
        waited = {e: {} for e in self.ENGS}
        waited_dma = {e: set() for e in self.ENGS}
        for rec in ops:
            ce = rec["eng"]
            need = {}
            need_dma = []
            for d in rec["deps"]:
                p = ops[d]
                if p["dma"]:
                    if d not in waited_dma[ce]:
                        need_dma.append(d)
                else:
                    pe_ = p["eng"]
                    if pe_ == "pe" and ce == "pe":
                        continue
                    if p["idx"] > waited[ce].get(pe_, -1):
                        need[pe_] = max(need.get(pe_, -1), p["idx"])
            for pe_, ix in need.items():
                waited[ce][pe_] = ix
            for d in need_dma:
                waited_dma[ce].add(d)
            rec["w_eng"] = need
            rec["w_dma"] = sorted(need_dma)
            rec["sig"] = False
        for rec in ops:
            for pe_, ix in rec["w_eng"].items():
                self.per_eng[pe_][ix]["sig"] = True
        for e in self.ENGS:
            comp = [r for r in self.per_eng[e] if not r["dma"]]
            if comp:
                comp[-1]["sig"] = True
        nsig = {}
        for e in self.ENGS:
            c = 0
            for r in self.per_eng[e]:
                if (not r["dma"]) and r["sig"]:
                    r["sig_n"] = c
                    c += 1
            nsig[e] = c
        stack = contextlib.ExitStack()
        sems = {}
        for e in self.ENGS:
            nep = (nsig[e] + EPOCH - 1) // EPOCH
            sems[e] = [stack.enter_context(nc.semaphore(f"s_{e}_{i}")) for i in range(nep)]
        dsems = {}
        for e in self.ENGS:
            if self.dma_count[e]:
                dsems[e] = [stack.enter_context(nc.semaphore(f"d_{e}_{i}")) for i in range(NDMASEM)]

        def sem_for(e, n):
            return sems[e][n // EPOCH], (n % EPOCH) + 1

        def dma_sem_for(rec):
            i = rec["dma_i"]
            return dsems[rec["eng"]][i % NDMASEM], 16 * (i // NDMASEM + 1)

        def run_engine(e, eng):
            for rec in self.per_eng[e]:
                for pe_, ix in rec["w_eng"].items():
                    s, v = sem_for(pe_, self.per_eng[pe_][ix]["sig_n"])
                    eng.wait_ge(s, v)
                for d in rec["w_dma"]:
                    s, v = dma_sem_for(ops[d])
                    eng.wait_ge(s, v)
                if rec["dma"]:
                    s, v = dma_sem_for(rec)
                    if v > 16:
                        eng.wait_ge(s, v - 16)
                    rec["fn"](eng).then_inc(s, 16)
                else:
                    ins = rec["fn"](eng)
                    if rec["sig"]:
                        s, v = sem_for(e, rec["sig_n"])
                        ins.then_inc(s, 1)
            if e == final_wait_eng:
                for e2 in self.ENGS:
                    n = self.dma_count[e2]
                    for j in range(min(n, NDMASEM)):
                        cnt = (n - 1 - j) // NDMASEM + 1
                        eng.wait_ge(dsems[e2][j], 16 * cnt)
                    if nsig[e2]:
                        s, v = sem_for(e2, nsig[e2] - 1)
                        eng.wait_ge(s, v)

        with stack:
            with nc.Block() as block:
                @block.tensor
                def _(eng):
                    run_engine("pe", eng)

                @block.scalar
                def _(eng):
                    run_engine("act", eng)

                @block.vector
                def _(eng):
                    run_engine("dve", eng)

                @block.gpsimd
                def _(eng):
                    run_engine("pool", eng)

                @block.sync
                def _(eng):
                    run_engine("sp", eng)


PIECES = ["mq", "mk", "mv", "mo", "mz", "aq", "ak", "av", "az",
          "gm0", "ga0", "wb0", "gm1", "ga1", "wb1", "wout0", "wout1"]
PIDX = {n: i for i, n in enumerate(PIECES)}
W_COL = dict(mq=0, mk=512, mv=1024, mo=1536, mz=2048, aq=2568, ak=3080, av=3592, az=4104,
             gm0=4616, gm1=5128, ga0=5640, ga1=6152)
FMC = dict(mq=0, mk=4, aq=8, ak=12)


def build_program():
    nc = bass.Bass("TRN2", target_bir_lowering=False)
    S = Sched(nc)

    def din(name, shape, dt=F32):
        return nc.dram_tensor(name, list(shape), dt, kind="ExternalInput").ap()

    def dout(name, shape, dt=F32):
        return nc.dram_tensor(name, list(shape), dt, kind="ExternalOutput").ap()

    x_p = din("x_p", [2, SEQ, D]); x_s = din("x_s", [2, TS, D])
    st_C = din("st_C", [2, 4, 128, 128]); st_n = din("st_n", [2, 4, 128]); st_m = din("st_m", [2, 4])
    st_conv = din("st_conv", [2, 3, D]); ck = din("ck", [2, 512, 512]); cv = din("cv", [2, 512, 512])
    w_all = din("w_all", [17, 128, 4096]); w_g = din("w_g", [128, 64])
    b_fm = din("b_fm", [128, 32]); b_g = din("b_g", [4, 2]); b_tm = din("b_tm", [5, 512]); sel5_d = din("sel5", [5, 5 * 128])
    gx_d = din("gx", [128, 8]); cw_d = din("cw", [128, 32]); cb_d = din("cb", [1, D])
    mg_d = din("mg", [128, 4]); gqk_d = din("gqk", [128, 2])
    relb_t = din("relb_t", [128, 8 * 2 * 128]); relb_c = din("relb_c", [128, 8])
    relb_s3 = din("relb_s3", [128, 8 * 16]); relb_sn = din("relb_sn", [16, 8 * 16])
    ident_d = din("ident", [128, 128]); cmask_d = din("cmask", [128, 128]); blk_d = din("blk64", [128, 128])
    vmask_d = din("vmask", [128, 3 * 128])
    wscr = nc.dram_tensor("wscr", [17, 128, 4096], BF16, kind="Internal").ap()

    y_p = dout("y_p", [2, SEQ, D]); y_s = dout("y_s", [2, TS, D])
    C_p = dout("C_p", [2, 4, 128, 128]); n_p = dout("n_p", [2, 4, 128]); m_p = dout("m_p", [2, 4])
    conv_p = dout("conv_p", [2, 3, D]); k_p = dout("k_p", [2, 512, 512]); v_p = dout("v_p", [2, 512, 512])
    C_s = dout("C_s", [2, 4, 128, 128]); n_s = dout("n_s", [2, 4, 128]); m_s = dout("m_s", [2, 4])
    conv_s = dout("conv_s", [2, 3, D]); k_s = dout("k_s", [2, TS, 512]); v_s = dout("v_s", [2, TS, 512])

    st = contextlib.ExitStack()

    def sb(name, shape, dt):
        return st.enter_context(nc.sbuf_tensor("sb_" + name, list(shape), dt))

    def ps(name, shape, dt):
        return st.enter_context(nc.psum_tensor("ps_" + name, list(shape), dt))

    with st:
        ident_f = sb("ident_f", [128, 128], F32)
        identb = sb("identb", [128, 128], BF16)
        cmask = sb("cmask", [128, 128], BF16)
        blk64 = sb("blk64", [128, 128], BF16)
        half_row = sb("half_row", [1, 512], BF16)
        bfm = sb("bfm", [128, 32], F32)
        bfm_h = sb("bfm_h", [128, 32], F32)
        bg = sb("bg", [4, 2], F32)
        nbg = sb("nbg", [4, 2], F32)
        btm = sb("btm", [5, 512], BF16)
        sel5 = sb("sel5", [5, 5 * 128], BF16)
        gx = sb("gx", [128, 8], F32)
        cw = sb("cw", [128, 32], F32)
        cbrow = sb("cbrow", [1, D], BF16)
        mg = sb("mg", [128, 4], F32)
        gqk = sb("gqk", [128, 2], F32)
        diagW = sb("diagW", [128, 32 * 128], BF16)
        wg = sb("wg", [128, 64], BF16)
        EBM = sb("EBM", [128, 8 * 2 * 128], BF16)
        EBS3 = sb("EBS3", [128, 8 * 16], BF16)
        EBSn = sb("EBSn", [16, 8 * 16], BF16)
        relc = sb("relc", [128, 8], F32)
        vmk = sb("vmk", [128, 384], BF16)
        wbuf = [sb(f"wbuf{i}", [128, 4096], BF16) for i in range(NWB)]
        xin = sb("xin", [128, 2 * D], F32)
        ybuf = sb("ybuf", [128, 3 * 512], F32)
        xs = sb("xs", [128, D], BF16)
        ssx = sb("ssx", [128, 4], F32)
        rsx = sb("rsx", [128, 4], F32)
        hT2 = [sb(f"hT{i}", [128, 8 * 512], BF16) for i in range(2)]
        zbuf = sb("zbuf", [128, 8 * 520], BF16)
        convo = sb("convo", [128, 2 * 8 * 3], F32)
        qkT = sb("qkT", [128, 8 * 512], BF16)
        tht = sb("tht", [128, 512], BF16)
        vaug = sb("vaug", [128, 4 * 4 * 129], BF16)
        vu = sb("vu", [128, 4 * 129], BF16)
        tho = sb("tho", [128, 4 * 512], BF16)
        thz = sb("thz", [128, 512], BF16)
        scr_mz = sb("scr_mz", [128, 512], F32)
        scr_a = sb("scr_a", [128, 512], F32)
        stage = sb("stage", [128, 1024], F32)
        scr_g1 = stage[:, 0:512]
        scr_g2 = stage[:, 512:1024]
        GM = sb("GM", [128, 4 * 512], BF16)
        GA = sb("GA", [128, 4 * 512], BF16)
        thg = sb("thg", [128, 8 * 512], BF16)
        qraw2 = [sb(f"qraw{i}", [128, 512], BF16) for i in range(2)]
        qsq2 = [sb(f"qsq{i}", [128, 512], BF16) for i in range(2)]
        rstd = sb("rstd", [128, 512], F32)
        aqT = sb("aqT", [128, 8 * 512], BF16)
        KTr = sb("KTr", [128, 2 * 4 * 512], BF16)
        Vr = sb("Vr", [128, 2 * 4 * 8 * 65], BF16)
        PT = sb("PT", [128, 40 * 128], BF16)
        ha = sb("ha", [128, 512], BF16)
        haT = sb("haT", [128, 4 * 512], BF16)
        hm = sb("hm", [128, 512], BF16)
        hmT2 = [sb(f"hmT{i}", [128, 4 * 512], BF16) for i in range(2)]
        ktok = sb("ktok", [128, 512], BF16)
        SW = sb("SW", [128, 512], BF16)
        C_f = sb("C_f", [128, 4 * 129], F32)
        Cdec_b = sb("Cdec_b", [128, 4 * 129], BF16)
        mixT = sb("mixT", [128, 8 * 512], BF16)
        R1 = sb("R1", [4, 512], F32)
        R2 = sb("R2", [4, 512], F32)
        R3 = sb("R3", [4, 512], F32)
        ones4 = sb("ones4", [4, 128], F32)
        Bext = sb("Bext", [4, 544], F32)
        Gext = sb("Gext", [4, 544], F32)
        darg = sb("darg", [4, 4], F32)
        dec = sb("dec", [4, 4], F32)
        decd = sb("decd", [4, 32], F32)
        mfin = sb("mfin", [4, 1], F32)
        gcols = sb("gcols", [128, 4 * 16], F32)
        dcol = sb("dcol", [128, 4], F32)
        rcol = sb("rcol", [128, 4], F32)
        sscol = sb("sscol", [128, 4], F32)
        tcol = sb("tcol", [128, 4], F32)
        t2col = sb("t2col", [128, 4], F32)
        rho = sb("rho", [128, 4], F32)
        rden = sb("rden", [128, 8], F32)
        pA = ps("pA", [128, 512], F32); pB = ps("pB", [128, 512], F32)
        pT = ps("pT", [128, 1024], BF16)
        pS = ps("pS", [128, 512], F32); pS2 = ps("pS2", [128, 512], F32)
        pX0 = ps("pX0", [128, 512], F32); pX1 = ps("pX1", [128, 512], F32)
        pG = ps("pG", [128, 512], F32)
        pX = [pX0, pX1]
        pSS = [pS, pS2]
        pSS3 = [pS, pS2, pG]
        sc_i = [0]
        acc = [(pA, "pA"), (pB, "pB")]
        acc_i = [0]

        acc3 = acc + [(pS2, "pS2")]
        acc3_i = [0]

        def next_acc(n=2):
            if n == 3:
                r = acc3[acc3_i[0] % 3]
                acc3_i[0] += 1
                return r
            r = acc[acc_i[0] % 2]
            acc_i[0] += 1
            return r

        def v3(ap, a):
            return ap.rearrange("p (a b) -> p a b", a=a)

        def bc(ap, shape, axis):
            return ap.unsqueeze(axis).to_broadcast(list(shape))

        def ld(dst, src, wk, eng="sp", rk=(), **kw):
            S.op(eng, lambda e, d=dst, s=src, kw=kw: e.dma_start(out=d, in_=s, **kw), reads=list(rk), writes=list(wk), dma=True)

        ld(ident_f[:], ident_d, ["ident_f"])
        ld(stage[:, 0:128], cmask_d, ["st_a"])
        ld(stage[:, 128:256], blk_d, ["st_b"])
        ld(stage[:, 256:640], vmask_d, ["st_c"])
        ld(bfm[:], b_fm, ["bfm"]); ld(bg[:], b_g, ["bg"])
        ld(gx[:], gx_d, ["gx"]); ld(cw[:], cw_d, ["cw"])
        ld(mg[:], mg_d, ["mg"]); ld(gqk[:], gqk_d, ["gqk"]); ld(relc[:], relb_c, ["relc"])
        S.op("dve", lambda e: e.tensor_copy(out=identb[:], in_=ident_f[:]), reads=["ident_f"], writes=["identb"])
        S.op("dve", lambda e: e.tensor_copy(out=cmask[:], in_=stage[:, 0:128]), reads=["st_a"], writes=["cmask"])
        S.op("dve", lambda e: e.tensor_copy(out=blk64[:], in_=stage[:, 128:256]), reads=["st_b"], writes=["blk64"])
        S.op("dve", lambda e: e.tensor_copy(out=vmk[:], in_=stage[:, 256:640]), reads=["st_c"], writes=["vmk"])
        S.op("pool", lambda e: e.memset(half_row[:], 0.5), writes=["half_row"])
        S.op("pool", lambda e: e.memset(ones4[:], 1.0), writes=["ones4"])
        S.op("pool", lambda e: e.memset(aqT[:], 0.0), writes=[f"aq{r}" for r in range(4)])
        S.op("pool", lambda e: e.memset(v3(vaug[:], 16)[:, :, 128:129], 1.0), writes=[f"vaug{j}" for j in range(4)])
        S.op("dve", lambda e: e.tensor_scalar(out=bfm_h[:], in0=bfm[:], scalar1=0.5, scalar2=None, op0=ALU.mult),
             reads=["bfm"], writes=["bfm_h"])
        S.op("dve", lambda e: e.tensor_scalar(out=nbg[:], in0=bg[:], scalar1=-1.0, scalar2=None, op0=ALU.mult),
             reads=["bg"], writes=["nbg"])
        S.op("dve", lambda e: e.tensor_scalar(out=cw[:], in0=cw[:], scalar1=0.5, scalar2=None, op0=ALU.mult),
             reads=["cw"], writes=["cw"])
        for i in range(32):
            S.op("pool" if i % 2 else "dve",
                 lambda e, i=i: e.tensor_scalar(out=diagW[:, i * 128:(i + 1) * 128], in0=ident_f[:], scalar1=cw[:, i:i + 1],
                                                scalar2=None, op0=ALU.mult),
                 reads=["ident_f", "cw"], writes=[f"diagW{i}"])
        S.op("dve", lambda e: e.tensor_scalar(out=relc[:], in0=relc[:], scalar1=-1.0, scalar2=None, op0=ALU.mult),
             reads=["relc"], writes=["relc"])
        ld(wg[:], w_g, ["wg", "castchain"], eng="pool")
        ld(btm[:], b_tm, ["btm", "castchain"], eng="pool")
        ld(sel5[:], sel5_d, ["sel5", "castchain"], eng="pool")
        ld(cbrow[:], cb_d, ["cbrow", "castchain"], eng="pool")
        EBMv = EBM[:].rearrange("p (h j q) -> p h j q", h=8, j=2)
        vm = v3(vmk[:], 3)
        def build_ebm():
            for hh in range(2):
                ld(stage[:, :], relb_t[:, hh * 1024:(hh + 1) * 1024], ["scr_g1", "scr_g2"], rk=["cmask", "blk64", "vmk"])
                for h4 in range(4):
                    h = hh * 4 + h4
                    S.op("act", lambda e, h=h, h4=h4: e.activation(out=stage[:, h4 * 256:(h4 + 1) * 256], in_=stage[:, h4 * 256:(h4 + 1) * 256],
                                                                   func=AF.Exp, bias=relc[:, h:h + 1], scale=1.0),
                         reads=["scr_g1", "scr_g2", "relc"], writes=["scr_g1", "scr_g2"])
                    S.op("dve", lambda e, h=h, h4=h4: e.tensor_tensor(out=EBMv[:, h, :, :], in0=v3(stage[:, h4 * 256:(h4 + 1) * 256], 2),
                                                                      in1=vm[:, 1:3, :], op=ALU.mult),
                         reads=["scr_g1", "scr_g2", "vmk"], writes=["EBM"])
            ld(stage[:, 0:128], relb_s3, ["scr_g1", "scr_g2"])
            ld(stage[0:16, 128:256], relb_sn, ["scr_g1", "scr_g2"])
            for h in range(8):
                S.op("act", lambda e, h=h: e.activation(out=EBS3[:, h * 16:(h + 1) * 16], in_=stage[:, h * 16:(h + 1) * 16],
                                                        func=AF.Exp, bias=relc[:, h:h + 1], scale=1.0),
                     reads=["scr_g1", "scr_g2", "relc"], writes=["EBS3"])
                S.op("act", lambda e, h=h: e.activation(out=EBSn[:, h * 16:(h + 1) * 16], in_=stage[0:16, 128 + h * 16:128 + (h + 1) * 16],
                                                        func=AF.Exp, bias=relc[0:16, h:h + 1], scale=1.0),
                     reads=["scr_g1", "scr_g2", "relc"], writes=["EBSn"])

        zv = v3(zbuf[:], 8); qkv = v3(qkT[:], 8)
        hTv2 = [v3(t[:], 8) for t in hT2]
        hmTv2 = [v3(t[:], 4) for t in hmT2]
        vaugv = vaug[:].rearrange("p (j h c) -> p j h c", j=4, h=4)
        vuv = v3(vu[:], 4)
        GMv = v3(GM[:], 4); GAv = v3(GA[:], 4); thov = v3(tho[:], 4); thgv = v3(thg[:], 8)
        aqv = v3(aqT[:], 8)
        KTv = KTr[:].rearrange("p (s r t) -> p s r t", s=2, r=4)
        Vv = Vr[:].rearrange("p (s j h c) -> p s j h c", s=2, j=4, h=8)
        PTv = PT[:].rearrange("p (h j q) -> p h j q", h=8, j=5)
        haTv = v3(haT[:], 4)
        Cfv = v3(C_f[:], 4); Cdbv = v3(Cdec_b[:], 4)
        mixv = v3(mixT[:], 8)
        wgv = v3(wg[:], 8)
        diagv = diagW[:].rearrange("p (c j q) -> p c j q", c=8, j=4)
        gcv = v3(gcols[:], 4)
        xinv = v3(xin[:], 2); ybv = v3(ybuf[:], 3)
        convov = convo[:].rearrange("p (g c r) -> p g c r", g=2, c=8)
        goff = [0, 520]
        xi_i = [0]
        yb_i = [0]
        PTK = [f"PT{nb}" for nb in range(10)]

        piece_seq = []
        piece_pos = [0]
        piece_loaded = [0]

        cast_done = set()

        def ensure_cast(name):
            if name not in cast_done:
                cast_done.add(name)
                g = PIDX[name]
                ld(wscr[g], w_all[g], [f"wscr{g}", "castchain"], eng="pool")

        def use_piece(name):
            if S.dry:
                return wbuf[0], "wbuf0"
            k = piece_pos[0]
            assert piece_seq[k] == name, (k, piece_seq[k], name)
            piece_pos[0] += 1
            while piece_loaded[0] < min(len(piece_seq), k + NWB):
                q = piece_loaded[0]
                for qq in range(q, min(len(piece_seq), q + 3)):
                    ensure_cast(piece_seq[qq])
                slot = q % NWB
                if not (PROBE_NOLOAD and q >= NWB):
                    ld(wbuf[slot][:], wscr[PIDX[piece_seq[q]]], [f"wbuf{slot}"], rk=[f"wscr{PIDX[piece_seq[q]]}"])
                piece_loaded[0] += 1
            slot = k % NWB
            return wbuf[slot], f"wbuf{slot}"

        class Tile:
            pass

        def make_tile(kind, b, ti, idx):
            T = Tile()
            if kind == "p":
                subs = [dict(b=b, t0=512 * ti + 128 * j, n=128, c0=128 * j) for j in range(4)]
                groups = [dict(b=b, c0=0, n=512, zoff=0, subs=[0, 1, 2, 3], gi=0)]
                NT = 512
                first = ti == 0
                last = ti == 3
                xsrc = x_p
            else:
                subs = [dict(b=bb, t0=0, n=TS, c0=TS * bb) for bb in range(2)]
                groups = [dict(b=bb, c0=TS * bb, n=TS, zoff=19 * bb, subs=[bb], gi=bb) for bb in range(2)]
                NT = 2 * TS
                first = True
                last = True
                xsrc = x_s
            nsub = len(subs)
            par = idx % 2
            hTv = hTv2[par]
            hmTv = hmTv2[par]
            hTk = [f"hT{par}_{j}" for j in range(nsub)]
            hmk = [f"hmT{par}_{j}" for j in range(nsub)]
            hak = [f"haT{j}" for j in range(nsub)]
            cur_slot = ti % 2 if kind == "p" else 1

            def g_xp():
                if first:
                    for g in groups:
                        gi = g["gi"]
                        o = goff[gi]
                        S.op("pool", lambda e, o=o: e.memset(Bext[:, o:o + 1], 0.0), writes=[f"Bext{gi}"])
                        if kind == "p":
                            S.op("pool", lambda e, o=o: e.memset(Gext[:, o:o + 1], 0.0), writes=[f"Gext{gi}"])
                            S.op("pool", lambda e: e.memset(zv[:, :, 0:3], 0.0), writes=[f"z{c}" for c in range(8)])
                        else:
                            ld(Gext[:, o:o + 1], st_m[g["b"]].rearrange("(h o) -> h o", o=1), [f"Gext{gi}"])
                            ld(stage[0:3, :], st_conv[g["b"]], ["scr_g1", "scr_g2"], rk=["EBS3", "EBSn"])
                            for ch in range(8):
                                S.op("pe", lambda e, ch=ch: e.matmul(pG[:, ch * 3:(ch + 1) * 3], lhsT=stage[0:3, ch * 128:(ch + 1) * 128],
                                                                     rhs=ident_f[0:3, 0:3], start=True, stop=True),
                                     reads=["scr_g1", "scr_g2", "ident_f"], writes=["pG"])
                            S.op("dve", lambda e, zo=g["zoff"]: e.tensor_copy(out=zv[:, :, zo:zo + 3], in_=v3(pG[:, 0:24], 8)),
                                 reads=["pG"], writes=[f"z{c}" for c in range(8)])
                    if kind == "p":
                        S.op("pool", lambda e: e.memset(C_f[:], 0.0), writes=["C_f"])
                    yield
                for j, su in enumerate(subs):
                    n = su["n"]
                    xb = xi_i[0] % 2
                    xi_i[0] += 1
                    xk = f"xin{xb}"
                    ld(xinv[0:n, xb, :], xsrc[su["b"], su["t0"]:su["t0"] + n, :], [xk])
                    S.op("act", lambda e, j=j, n=n, xb=xb: e.activation(out=xs[0:n, :], in_=xinv[0:n, xb, :], func=AF.Square,
                                                                        accum_out=ssx[0:n, j:j + 1]),
                         reads=[xk], writes=["xs", f"ssx{j}"])
                    S.op("act", lambda e, j=j, n=n: e.activation(out=rsx[0:n, j:j + 1], in_=ssx[0:n, j:j + 1], func=AF.Ln,
                                                                 bias=EPS, scale=1.0 / D),
                         reads=[f"ssx{j}"], writes=[f"rsx{j}"])
                    S.op("act", lambda e, j=j, n=n: e.activation(out=rsx[0:n, j:j + 1], in_=rsx[0:n, j:j + 1], func=AF.Exp, scale=-0.5),
                         reads=[f"rsx{j}"], writes=[f"rsx{j}"])
                    S.op("dve", lambda e, j=j, n=n, xb=xb: e.tensor_scalar(out=xs[0:n, :], in0=xinv[0:n, xb, :], scalar1=rsx[0:n, j:j + 1],
                                                                           scalar2=None, op0=ALU.mult),
                         reads=[xk, f"rsx{j}"], writes=["xs"])
                    yield
                    yield
                    for kc in range(8):
                        S.op("pe", lambda e, kc=kc, n=n: e.transpose(pT[:, kc * 128:kc * 128 + n], xs[0:n, kc * 128:(kc + 1) * 128],
                                                                     identb[0:n, 0:n]),
                             reads=["xs", "identb"], writes=["pT"])
                    S.op("dve", lambda e, su=su, n=n: e.tensor_tensor(out=hTv[:, :, su["c0"]:su["c0"] + n],
                                                                      in0=v3(pT[:], 8)[:, :, 0:n],
                                                                      in1=bc(gx[:], [128, 8, n], 2), op=ALU.mult),
                         reads=["pT", "gx"], writes=[hTk[j]])
                    yield
                deferred = []
                for gsel, (pb, pk) in enumerate([(pA, "pA"), (pB, "pB")]):
                    for kc in range(8):
                        S.op("pe", lambda e, kc=kc, gsel=gsel, pb=pb: e.matmul(pb[0:4, 0:NT], lhsT=wgv[:, kc, gsel * 4:gsel * 4 + 4],
                                                                               rhs=hTv[:, kc, 0:NT], start=(kc == 0), stop=(kc == 7)),
                             reads=hTk + ["wg"], writes=[pk])
                acc_i[0] = 0
                S.op("act", lambda e: e.activation(out=R1[:, 0:NT], in_=pA[0:4, 0:NT], func=AF.Identity, bias=bg[:, 0:1], scale=1.0),
                     reads=["pA", "bg"], writes=["R1"])
                S.op("act", lambda e: e.activation(out=R2[:, 0:NT], in_=pB[0:4, 0:NT], func=AF.Exp, bias=nbg[:, 1:2], scale=-1.0),
                     reads=["pB", "nbg"], writes=["R2"])
                S.op("act", lambda e: e.activation(out=R2[:, 0:NT], in_=R2[:, 0:NT], func=AF.Ln, bias=1.0, scale=1.0),
                     reads=["R2"], writes=["R2"])
                yield
                for g in groups:
                    gi, c0, n = g["gi"], g["c0"], g["n"]
                    o = goff[gi]
                    L = subs[g["subs"][0]]["n"]
                    nch = n // L
                    Bk, Gk = f"Bext{gi}", f"Gext{gi}"
                    Bn = Bext[:, o + 1:o + 1 + n]
                    Gn = Gext[:, o + 1:o + 1 + n]
                    S.op("dve", lambda e, Bn=Bn, c0=c0, n=n, o=o: e.tensor_tensor_scan(out=Bn, data0=ones4[:, 0:1].to_broadcast([4, n]), data1=R2[:, c0:c0 + n],
                                                                                       initial=Bext[:, o:o + 1], op0=ALU.mult, op1=ALU.subtract),
                         reads=["ones4", "R2", Bk], writes=[Bk])
                    S.op("dve", lambda e, Bn=Bn, c0=c0, n=n: e.tensor_tensor(out=R1[:, c0:c0 + n], in0=R1[:, c0:c0 + n], in1=Bn, op=ALU.subtract),
                         reads=["R1", Bk], writes=["R1"])
                    S.op("dve", lambda e, Gn=Gn, c0=c0, n=n, o=o: e.tensor_tensor_scan(out=Gn, data0=ones4[:, 0:1].to_broadcast([4, n]), data1=R1[:, c0:c0 + n],
                                                                                       initial=Gext[:, o:o + 1], op0=ALU.mult, op1=ALU.max),
                         reads=["ones4", "R1", Gk], writes=[Gk])
                    yield
                    gend = Gn.rearrange("p (c l) -> p c l", l=L)[:, :, L - 1:L]
                    gprev = Gext[:, o:o + n].rearrange("p (c l) -> p c l", l=L)[:, :, 0:1]
                    S.op("dve", lambda e, c0=c0, n=n, L=L, gend=gend, nch=nch: e.tensor_tensor(
                        out=R1[:, c0:c0 + n].rearrange("p (c l) -> p c l", l=L),
                        in0=R1[:, c0:c0 + n].rearrange("p (c l) -> p c l", l=L),
                        in1=gend.to_broadcast([4, nch, L]), op=ALU.subtract),
                        reads=["R1", Gk], writes=["R1"])
                    S.op("act", lambda e, c0=c0, n=n: e.activation(out=R1[:, c0:c0 + n], in_=R1[:, c0:c0 + n], func=AF.Exp, bias=LNCK, scale=1.0),
                         reads=["R1"], writes=["R1"])
                    S.op("dve", lambda e, Bn=Bn, c0=c0, n=n, L=L, gend=gend, nch=nch: e.tensor_tensor(
                        out=R3[:, c0:c0 + n].rearrange("p (c l) -> p c l", l=L),
                        in0=Bn.rearrange("p (c l) -> p c l", l=L),
                        in1=gend.to_broadcast([4, nch, L]), op=ALU.add),
                        reads=[Bk, Gk], writes=["R3"])
                    S.op("act", lambda e, c0=c0, n=n: e.activation(out=R3[:, c0:c0 + n], in_=R3[:, c0:c0 + n], func=AF.Exp, scale=-1.0),
                         reads=["R3"], writes=["R3"])
                    S.op("dve", lambda e, gend=gend, gprev=gprev, nch=nch: e.tensor_tensor(
                        out=darg[:, 0:nch].unsqueeze(2), in0=gprev, in1=gend, op=ALU.subtract),
                        reads=[Gk], writes=["darg"])
                    S.op("act", lambda e, nch=nch: e.activation(out=dec[:, 0:nch], in_=darg[:, 0:nch], func=AF.Exp),
                         reads=["darg"], writes=["dec"])
                    S.op("dve", lambda e, nch=nch, gi=gi: e.tensor_tensor(out=v3(decd[:, gi * 16:gi * 16 + nch * 4], nch),
                                                                   in0=bc(ident_f[0:4, 0:4], [4, nch, 4], 1),
                                                                   in1=bc(dec[:, 0:nch], [4, nch, 4], 2), op=ALU.mult),
                         reads=["ident_f", "dec"], writes=[f"decd{gi}"])
                    yield
                    deferred.append((g, L, nch, c0))
                    if last:
                        S.op("dve", lambda e, o=o, n=n: e.tensor_tensor(out=mfin[:], in0=Bext[:, o + n:o + n + 1], in1=Gext[:, o + n:o + n + 1],
                                                                        op=ALU.add),
                             reads=[Bk, Gk], writes=["mfin"])
                        mo_ = m_p if kind == "p" else m_s
                        ld(mo_[g["b"]].rearrange("(h o) -> h o", o=1), mfin[:], [], rk=["mfin"])
                    S.op("pool", lambda e, o=o, n=n: e.tensor_copy(out=Bext[:, o:o + 1], in_=Bext[:, o + n:o + n + 1]),
                         reads=[Bk], writes=[Bk])
                    S.op("pool", lambda e, o=o, n=n: e.tensor_copy(out=Gext[:, o:o + 1], in_=Gext[:, o + n:o + n + 1]),
                         reads=[Gk], writes=[Gk])
                    yield
                yield from g_pieces(["mq", "mk"])
                for (g, L, nch, c0) in deferred:
                    for c in range(nch):
                        sj = g["subs"][c]
                        cc = c0 + c * L
                        S.op("pe", lambda e, cc=cc, L=L, sj=sj: e.matmul(pG[0:L, sj * 16:sj * 16 + 4], lhsT=R1[:, cc:cc + L],
                                                                         rhs=ident_f[0:4, 0:4], start=True, stop=True),
                             reads=["R1", "ident_f"], writes=["pG"])
                        S.op("pe", lambda e, cc=cc, L=L, sj=sj: e.matmul(pG[0:L, sj * 16 + 4:sj * 16 + 8], lhsT=R3[:, cc:cc + L],
                                                                         rhs=ident_f[0:4, 0:4], start=True, stop=True),
                             reads=["R3", "ident_f"], writes=["pG"])
                        S.op("pe", lambda e, c=c, sj=sj, gi=g["gi"]: e.matmul(pG[:, sj * 16 + 8:sj * 16 + 12], lhsT=ones4[:, 0:128],
                                                                              rhs=decd[:, gi * 16 + c * 4:gi * 16 + (c + 1) * 4], start=True, stop=True),
                             reads=[f"decd{g['gi']}", "ones4"], writes=["pG"])
                    s0, s1 = g["subs"][0], g["subs"][-1] + 1
                    S.op("dve", lambda e, s0=s0, s1=s1, L=L: e.tensor_copy(out=gcv[0:L, s0:s1, 0:8], in_=v3(pG[0:L, 0:64], 4)[:, s0:s1, 0:8]),
                         reads=["pG"], writes=["gcols"])
                    S.op("dve", lambda e, s0=s0, s1=s1: e.tensor_copy(out=gcv[:, s0:s1, 8:12], in_=v3(pG[:, 0:64], 4)[:, s0:s1, 8:12]),
                         reads=["pG"], writes=["gcols"])
                    yield
                yield from g_pieces(["mv", "mo", "mz"])

            def g_pl():
                if first and kind == "p":
                    S.op("pool", lambda e: e.memset(KTv[:, 1], 0.0), writes=[f"KT1_{r}" for r in range(4)])
                    S.op("pool", lambda e: e.memset(Vv[:, 1], 0.0), writes=[f"V1_{j}" for j in range(4)])
                yield from g_pieces(["aq", "ak", "av", "az"])
                if last:
                    for j, su in enumerate(subs):
                        n, c0 = su["n"], su["c0"]
                        for r in range(4):
                            S.op("pe", lambda e, r=r, n=n, c0=c0: e.transpose(pT[0:n, r * 128:(r + 1) * 128],
                                                                              KTv[:, cur_slot, r, c0:c0 + n], identb[:, :]),
                                 reads=[f"KT{cur_slot}_{r}", "identb"], writes=["pT"])
                        S.op("dve", lambda e, n=n: e.tensor_copy(out=scr_a[0:n, :], in_=pT[0:n, 0:512]), reads=["pT"], writes=["scr_a"])
                        ko = k_p[su["b"], 128 * j:128 * j + n, :] if kind == "p" else k_s[su["b"], :, :]
                        ld(ko, scr_a[0:n, :], [], rk=["scr_a"])
                        yield

            def conv_tail(ch):
                pb2, pk2 = next_acc()
                for g in groups:
                    zo, c0, n = g["zoff"], g["c0"], g["n"]
                    for jt in range(4):
                        S.op("pe", lambda e, pb2=pb2, ch=ch, jt=jt, zo=zo, c0=c0, n=n: e.matmul(
                            pb2[:, c0:c0 + n], lhsT=diagv[:, ch, jt, :], rhs=zv[:, ch, zo + jt:zo + jt + n],
                            start=(jt == 0), stop=False), reads=[f"z{ch}", f"diagW{ch * 4 + jt}"], writes=[pk2])
                    S.op("pe", lambda e, pb2=pb2, ch=ch, c0=c0, n=n: e.matmul(
                        pb2[:, c0:c0 + n], lhsT=cbrow[0:1, ch * 128:(ch + 1) * 128], rhs=half_row[0:1, 0:n],
                        start=False, stop=True), reads=["cbrow", "half_row"], writes=[pk2])
                S.op("act", lambda e, pb2=pb2: e.activation(out=tht[:, 0:NT], in_=pb2[:, 0:NT], func=AF.Tanh),
                     reads=[pk2], writes=["tht"])
                S.op("dve", lambda e, pb2=pb2, ch=ch: e.scalar_tensor_tensor(out=qkv[:, ch, 0:NT], in0=tht[:, 0:NT], scalar=1.0,
                                                                             in1=pb2[:, 0:NT], op0=ALU.add, op1=ALU.mult),
                     reads=["tht", pk2], writes=[f"qk{ch}"])
                if kind == "p" and not last:
                    S.op("pool", lambda e, ch=ch: e.tensor_copy(out=zv[:, ch, 0:3], in_=zv[:, ch, 512:515]),
                         reads=[f"z{ch}"], writes=[f"z{ch}"])

            def qk_tail(name, c4):
                qb = c4 % 2
                S.op("pe", lambda e, qb=qb: e.matmul(pG[:, 0:NT], lhsT=blk64[:], rhs=qsq2[qb][:, 0:NT], start=True, stop=True),
                     reads=[f"qsq{qb}", "blk64"], writes=["pG"])
                S.op("act", lambda e: e.activation(out=rstd[:, 0:NT], in_=pG[:, 0:NT], func=AF.Ln, bias=EPS, scale=1.0 / 64),
                     reads=["pG"], writes=["rstd"])
                S.op("act", lambda e: e.activation(out=rstd[:, 0:NT], in_=rstd[:, 0:NT], func=AF.Exp, scale=-0.5),
                     reads=["rstd"], writes=["rstd"])
                if name == "aq":
                    for hh in range(2):
                        S.op("dve", lambda e, c4=c4, hh=hh, qb=qb: e.scalar_tensor_tensor(
                            out=aqv[hh * 64:hh * 64 + 64, 2 * c4 + hh, 0:NT], in0=qraw2[qb][hh * 64:hh * 64 + 64, 0:NT],
                            scalar=gqk[hh * 64:hh * 64 + 64, 0:1], in1=rstd[hh * 64:hh * 64 + 64, 0:NT],
                            op0=ALU.mult, op1=ALU.mult),
                            reads=[f"qraw{qb}", "gqk", "rstd"], writes=[f"aq{c4}"])
                else:
                    dst = KTv[:, cur_slot, c4, 0:NT]
                    S.op("dve", lambda e, dst=dst, qb=qb: e.scalar_tensor_tensor(out=dst, in0=qraw2[qb][:, 0:NT], scalar=gqk[:, 1:2],
                                                                                 in1=rstd[:, 0:NT], op0=ALU.mult, op1=ALU.mult),
                         reads=[f"qraw{qb}", "gqk", "rstd"], writes=[f"KT{cur_slot}_{c4}"])

            def g_pieces(names):
                for name in names:
                    NACC = 3 if name in ("aq", "ak", "av", "az") else 2
                    wt, wk = use_piece(name)
                    wv = v3(wt[:], 8)
                    if name in FMC:
                        for c4 in range(4):
                            ch = FMC[name] + c4
                            pb, pk = next_acc(NACC)
                            for kc in range(8):
                                S.op("pe", lambda e, pb=pb, wv=wv, kc=kc, c4=c4: e.matmul(pb[:, 0:NT], lhsT=wv[:, kc, c4 * 128:(c4 + 1) * 128],
                                                                                          rhs=hTv[:, kc, 0:NT], start=(kc == 0), stop=(kc == 7)),
                                     reads=hTk + [wk], writes=[pk])
                            if name in ("mq", "mk"):
                                for g in groups:
                                    zo, c0, n = g["zoff"], g["c0"], g["n"]
                                    S.op("act", lambda e, pb=pb, ch=ch, zo=zo, c0=c0, n=n: e.activation(
                                        out=zv[:, ch, zo + 3:zo + 3 + n], in_=pb[:, c0:c0 + n], func=AF.Identity,
                                        bias=bfm[:, ch:ch + 1], scale=1.0), reads=[pk, "bfm"], writes=[f"z{ch}"])
                                    if last:
                                        S.op("act", lambda e, pb=pb, ch=ch, c0=c0, n=n, gi=g["gi"]: e.activation(
                                            out=convov[:, gi, ch, :], in_=pb[:, c0 + n - 3:c0 + n], func=AF.Identity,
                                            bias=bfm[:, ch:ch + 1], scale=1.0), reads=[pk, "bfm"], writes=[f"convo{g['gi']}"])
                                if c4 > 0:
                                    conv_tail(ch - 1)
                                if c4 == 3:
                                    yield
                                    conv_tail(ch)
                            else:
                                qb = c4 % 2
                                S.op("act", lambda e, pb=pb, ch=ch, qb=qb: e.activation(out=qsq2[qb][:, 0:NT], in_=pb[:, 0:NT], func=AF.Square,
                                                                                        bias=bfm[:, ch:ch + 1], scale=1.0),
                                     reads=[pk, "bfm"], writes=[f"qsq{qb}", "pb_serial"])
                                S.op("dve", lambda e, pb=pb, ch=ch, qb=qb: e.tensor_tensor(out=qraw2[qb][:, 0:NT], in0=pb[:, 0:NT],
                                                                                           in1=bfm[:, ch:ch + 1].to_broadcast([128, NT]), op=ALU.add),
                                     reads=[pk, "bfm", "pb_serial"], writes=[f"qraw{qb}"])
                                if c4 > 0:
                                    qk_tail(name, c4 - 1)
                                if c4 == 3:
                                    yield
                                    qk_tail(name, 3)
                            yield
                    else:
                        tmi = TM_IDX[name]
                        for j, su in enumerate(subs):
                            n, c0 = su["n"], su["c0"]
                            pb, pk = next_acc(NACC)
                            for kc in range(8):
                                S.op("pe", lambda e, pb=pb, wv=wv, kc=kc, n=n, c0=c0: e.matmul(pb[0:n, :], lhsT=hTv[:, kc, c0:c0 + n],
                                                                                               rhs=wv[:, kc, :], start=(kc == 0), stop=False),
                                     reads=[hTk[j], wk], writes=[pk])
                            S.op("pe", lambda e, pb=pb, n=n, tmi=tmi: e.matmul(pb[0:n, :], lhsT=sel5[0:5, tmi * 128:tmi * 128 + n],
                                                                               rhs=btm[0:5, :], start=False, stop=True),
                                 reads=["sel5", "btm"], writes=[pk])
                            if name == "mv":
                                S.op("act", lambda e, pb=pb, n=n, j=j: e.activation(out=vaugv[0:n, j, :, 0:128], in_=v3(pb[0:n, :], 4),
                                                                                    func=AF.Copy), reads=[pk], writes=[f"vaug{j}"])
                            elif name == "mo":
                                S.op("act", lambda e, pb=pb, n=n, j=j: e.activation(out=thov[0:n, j, :], in_=pb[0:n, :], func=AF.Tanh, scale=0.5),
                                     reads=[pk], writes=[f"tho{j}"])
                            elif name == "mz":
                                S.op("act", lambda e, pb=pb, n=n: e.activation(out=thz[0:n, :], in_=pb[0:n, :], func=AF.Tanh, scale=0.5),
                                     reads=[pk], writes=["thz"])
                                S.op("dve", lambda e, pb=pb, n=n: e.scalar_tensor_tensor(out=scr_mz[0:n, :], in0=thz[0:n, :], scalar=1.0,
                                                                                         in1=pb[0:n, :], op0=ALU.add, op1=ALU.mult),
                                     reads=["thz", pk], writes=["scr_mz"])
                                S.op("dve", lambda e, n=n, j=j: e.scalar_tensor_tensor(out=GMv[0:n, j, :], in0=thov[0:n, j, :], scalar=1.0,
                                                                                       in1=scr_mz[0:n, :], op0=ALU.add, op1=ALU.mult),
                                     reads=[f"tho{j}", "scr_mz"], writes=[f"GM{j}"])
                            elif name == "av":
                                vdst = Vv[0:n, cur_slot, j, :, 0:64]
                                odst = Vv[0:n, cur_slot, j, :, 64:65]
                                vk = f"V{cur_slot}_{j}"
                                S.op("act", lambda e, pb=pb, n=n, vdst=vdst: e.activation(out=vdst, in_=v3(pb[0:n, :], 8), func=AF.Copy),
                                     reads=[pk], writes=[vk])
                                S.op("pool", lambda e, odst=odst: e.memset(odst, 1.0), writes=[vk])
                                if last:
                                    S.op("act", lambda e, pb=pb, n=n: e.activation(out=scr_a[0:n, :], in_=pb[0:n, :], func=AF.Copy),
                                         reads=[pk], writes=["scr_a"])
                                    vo = v_p[su["b"], 128 * j:128 * j + n, :] if kind == "p" else v_s[su["b"], :, :]
                                    ld(vo, scr_a[0:n, :], [], rk=["scr_a"])
                            elif name == "az":
                                S.op("act", lambda e, pb=pb, n=n: e.activation(out=thz[0:n, :], in_=pb[0:n, :], func=AF.Tanh, scale=0.5),
                                     reads=[pk], writes=["thz"])
                                S.op("dve", lambda e, pb=pb, n=n, j=j: e.scalar_tensor_tensor(out=GAv[0:n, j, :], in0=thz[0:n, :], scalar=1.0,
                                                                                              in1=pb[0:n, :], op0=ALU.add, op1=ALU.mult),
                                     reads=["thz", pk], writes=[f"GA{j}"])
                            yield
                    if name == "mk" and last:
                        co = conv_p if kind == "p" else conv_s
                        for g in groups:
                            for hf in range(2):
                                for c4 in range(4):
                                    S.op("pe", lambda e, hf=hf, c4=c4, gi=g["gi"]: e.matmul(pG[0:3, c4 * 128:(c4 + 1) * 128],
                                                                                            lhsT=convov[:, gi, hf * 4 + c4, :], rhs=ident_f[:, :],
                                                                                            start=True, stop=True),
                                         reads=[f"convo{g['gi']}", "ident_f"], writes=["pG"])
                                S.op("dve", lambda e, hf=hf: e.tensor_copy(out=stage[0:3, hf * 512:(hf + 1) * 512], in_=pG[0:3, 0:512]),
                                     reads=["pG"], writes=["scr_g1", "scr_g2"])
                            ld(co[g["b"]], stage[0:3, :], [], rk=["scr_g1", "scr_g2"])
                            yield

            def g_m():
                for j, su in enumerate(subs):
                    L, c0 = su["n"], su["c0"]
                    if kind == "s":
                        for h in range(4):
                            ld(Cfv[:, h, 0:128], st_C[su["b"], h], ["C_f"])
                        ld(stage[0:4, 0:128], st_n[su["b"]], ["scr_g1", "scr_g2"], rk=["EBS3", "EBSn"])
                        S.op("pe", lambda e: e.matmul(pS2[:, 0:4], lhsT=stage[0:4, 0:128], rhs=ident_f[0:4, 0:4], start=True, stop=True),
                             reads=["scr_g1", "scr_g2", "ident_f"], writes=["pS2"])
                        S.op("dve", lambda e: e.tensor_copy(out=Cfv[:, :, 128:129], in_=pS2[:, 0:4].unsqueeze(2)), reads=["pS2"], writes=["C_f"])
                    ucol = gcv[0:L, j, 0:4]
                    fcol = gcv[0:L, j, 4:8]
                    dcl = gcv[:, j, 8:12]
                    S.op("pool", lambda e, dcl=dcl: e.tensor_tensor(out=Cfv[:, :, :], in0=Cfv[:, :, :], in1=bc(dcl, [128, 4, 129], 2), op=ALU.mult),
                         reads=["C_f", "gcols"], writes=["C_f"])
                    S.op("act", lambda e: e.activation(out=Cdec_b[:], in_=C_f[:], func=AF.Copy), reads=["C_f"], writes=["Cdec_b"])
                    S.op("pool", lambda e, L=L, j=j, ucol=ucol: e.tensor_tensor(out=vuv[0:L, :, :], in0=vaugv[0:L, j, :, :],
                                                                                in1=bc(ucol, [L, 4, 129], 2), op=ALU.mult),
                         reads=[f"vaug{j}", "gcols"], writes=["vu"])
                    for h in range(4):
                        S.op("pe", lambda e, h=h, L=L, c0=c0: e.matmul(pS[0:L, h * 128:h * 128 + L], lhsT=qkv[:, 4 + h, c0:c0 + L],
                                                                       rhs=qkv[:, h, c0:c0 + L], start=True, stop=True),
                             reads=[f"qk{h}", f"qk{4 + h}"], writes=["pS"])
                    for h in range(4):
                        S.op("pe", lambda e, h=h, L=L, c0=c0: e.transpose(pT[0:L, 512 + h * 128:512 + (h + 1) * 128], qkv[:, 4 + h, c0:c0 + L],
                                                                          identb[:, :]),
                             reads=[f"qk{4 + h}", "identb"], writes=["pT"])
                    S.op("act", lambda e, L=L: e.activation(out=ktok[0:L, :], in_=pT[0:L, 512:1024], func=AF.Copy), reads=["pT"], writes=["ktok"])
                    yield
                    S.op("dve", lambda e, L=L: e.tensor_tensor(out=v3(SW[0:L, :], 4)[:, :, 0:L], in0=v3(pS[0:L, :], 4)[:, :, 0:L],
                                                               in1=bc(cmask[0:L, 0:L], [L, 4, L], 1), op=ALU.mult),
                         reads=["pS", "cmask"], writes=["SW"])
                    yield
                    for h in range(4):
                        px = pX[h // 2]
                        o0 = (h % 2) * 129
                        S.op("pe", lambda e, h=h, L=L, px=px, o0=o0: e.matmul(px[0:L, o0:o0 + 129], lhsT=v3(SW[0:L, :], 4)[:, h, 0:L],
                                                                              rhs=vuv[0:L, h, :], start=True, stop=False),
                             reads=["SW", "vu"], writes=[f"pX{h // 2}"])
                        S.op("pe", lambda e, h=h, L=L, px=px, o0=o0, c0=c0: e.matmul(px[0:L, o0:o0 + 129], lhsT=qkv[:, h, c0:c0 + L],
                                                                                     rhs=Cdbv[:, h, :], start=False, stop=True),
                             reads=[f"qk{h}", "Cdec_b"], writes=[f"pX{h // 2}"])
                    yield
                    for hb in range(2):
                        S.op("act", lambda e, hb=hb, L=L: e.activation(
                            out=dcol[0:L, hb * 2:hb * 2 + 2].unsqueeze(2), in_=v3(pX[hb][0:L, 0:258], 2)[:, :, 128:129], func=AF.Abs),
                            reads=[f"pX{hb}"], writes=["dcol"])
                    for h in range(4):
                        px = pX[h // 2]
                        o0 = (h % 2) * 129
                        S.op("act", lambda e, h=h, L=L, px=px, o0=o0: e.activation(out=hm[0:L, h * 128:(h + 1) * 128], in_=px[0:L, o0:o0 + 128],
                                                                                   func=AF.Square, accum_out=sscol[0:L, h:h + 1]),
                             reads=[f"pX{h // 2}"], writes=["hm", "sscol"])
                    S.op("dve", lambda e, L=L, fcol=fcol: e.tensor_tensor(out=dcol[0:L, :], in0=dcol[0:L, :], in1=fcol, op=ALU.max),
                         reads=["dcol", "gcols"], writes=["dcol"])
                    S.op("dve", lambda e, L=L: e.reciprocal(out=rcol[0:L, :], in_=dcol[0:L, :]), reads=["dcol"], writes=["rcol"])
                    S.op("dve", lambda e, L=L: e.tensor_tensor(out=tcol[0:L, :], in0=rcol[0:L, :], in1=rcol[0:L, :], op=ALU.mult),
                         reads=["rcol"], writes=["tcol"])
                    yield
                    S.op("dve", lambda e, L=L: e.tensor_tensor(out=tcol[0:L, :], in0=tcol[0:L, :], in1=sscol[0:L, :], op=ALU.mult),
                         reads=["tcol", "sscol"], writes=["tcol"])
                    S.op("act", lambda e, L=L: e.activation(out=t2col[0:L, :], in_=tcol[0:L, :], func=AF.Ln, bias=4.0 * EPS, scale=4.0 / 128),
                         reads=["tcol"], writes=["t2col"])
                    S.op("act", lambda e, L=L: e.activation(out=t2col[0:L, :], in_=t2col[0:L, :], func=AF.Exp, scale=-0.5),
                         reads=["t2col"], writes=["t2col"])
                    S.op("dve", lambda e, L=L: e.tensor_tensor(out=rho[0:L, :], in0=rcol[0:L, :], in1=t2col[0:L, :], op=ALU.mult),
                         reads=["rcol", "t2col"], writes=["rho"])
                    yield
                    for h in range(4):
                        px = pX[h // 2]
                        o0 = (h % 2) * 129
                        S.op("dve", lambda e, h=h, L=L, px=px, o0=o0, j=j: e.scalar_tensor_tensor(
                            out=hm[0:L, h * 128:(h + 1) * 128], in0=px[0:L, o0:o0 + 128], scalar=rho[0:L, h:h + 1],
                            in1=GMv[0:L, j, h * 128:(h + 1) * 128], op0=ALU.mult, op1=ALU.mult),
                            reads=[f"pX{h // 2}", "rho", f"GM{j}"], writes=["hm"])
                    yield
                    yield
                    for h in range(4):
                        px = pX[h // 2]
                        o0 = (h % 2) * 129
                        S.op("pe", lambda e, h=h, L=L, px=px, o0=o0: e.matmul(px[:, o0:o0 + 129], lhsT=ktok[0:L, h * 128:(h + 1) * 128],
                                                                              rhs=vuv[0:L, h, :], start=True, stop=True),
                             reads=["ktok", "vu"], writes=[f"pX{h // 2}"])
                    yield
                    for h in range(4):
                        S.op("pe", lambda e, h=h, L=L: e.transpose(pT[:, h * 128:h * 128 + L], hm[0:L, h * 128:(h + 1) * 128], identb[0:L, 0:L]),
                             reads=["hm", "identb"], writes=["pT"])
                    for hb in range(2):
                        S.op("dve", lambda e, hb=hb: e.tensor_tensor(out=Cfv[:, hb * 2:hb * 2 + 2, :], in0=Cfv[:, hb * 2:hb * 2 + 2, :],
                                                                     in1=v3(pX[hb][:, 0:258], 2), op=ALU.add),
                             reads=["C_f", f"pX{hb}"], writes=["C_f"])
                    S.op("dve", lambda e, L=L, c0=c0: e.tensor_tensor(out=hmTv[:, :, c0:c0 + L], in0=v3(pT[:, 0:512], 4)[:, :, 0:L],
                                                                      in1=bc(mg[:, 0:4], [128, 4, L], 2), op=ALU.mult),
                         reads=["pT", "mg"], writes=[hmk[j]])
                    if last and (kind == "s" or j == nsub - 1):
                        Co, no = (C_p, n_p) if kind == "p" else (C_s, n_s)
                        for h in range(4):
                            ld(Co[su["b"], h], Cfv[:, h, 0:128], [], rk=["C_f"])
                        S.op("pe", lambda e: e.matmul(pS2[0:4, 0:128], lhsT=Cfv[:, :, 128], rhs=ident_f[:, :], start=True, stop=True),
                             reads=["C_f", "ident_f"], writes=["pS2"])
                        S.op("dve", lambda e: e.tensor_copy(out=stage[0:4, 0:128], in_=pS2[0:4, 0:128]), reads=["pS2"], writes=["scr_g1", "scr_g2"])
                        ld(no[su["b"]], stage[0:4, 0:128], [], rk=["scr_g1", "scr_g2"])
                    yield

            def g_a():
                for j, su in enumerate(subs):
                    n, c0 = su["n"], su["c0"]
                    if kind == "p":
                        order = (0, 3, 4, 1, 2)
                        keyt = []
                        for jk in order:
                            gk = 4 * ti + j + jk - 4
                            keyt.append(((gk // 4) % 2, gk % 4))
                        PT6 = PT[:].rearrange("p (u pl j hh q) -> p u pl j hh q", u=2, pl=2, j=5, hh=2)

                        def pv_unit(hg, keyt=keyt):
                            for h in range(4 * hg, 4 * hg + 4):
                                px = pX[h // 4]
                                o0 = (h % 4) * 65
                                pl, hh = (h % 4) // 2, h % 2
                                for jj in range(5):
                                    slot, kt = keyt[jj]
                                    S.op("pe", lambda e, h=h, jj=jj, px=px, o0=o0, slot=slot, kt=kt, pl=pl, hh=hh, hg=hg: e.matmul(
                                        px[:, o0:o0 + 65], lhsT=PT6[:, hg, pl, jj, hh, :], rhs=Vv[:, slot, kt, h, :], start=(jj == 0), stop=(jj == 4)),
                                        reads=[f"PT{nb}" for nb in range(5 * hg, 5 * hg + 5)] + [f"V{slot}_{kt}"], writes=[f"pX{h // 4}"])

                        for hg in range(2):
                            for it in range(10):
                                pl, jj = it // 5, it % 5
                                pr = 2 * hg + pl
                                slot, kt = keyt[jj]
                                idx2 = 20 * hg + 2 * it
                                bank = sc_i[0] % 3
                                bk = ("pS", "pS2", "pG")[bank]
                                S.op("pe", lambda e, pr=pr, slot=slot, kt=kt, bank=bank, idx2=idx2, c0=c0: e.matmul(
                                    pSS3[bank][:, (idx2 % 4) * 128:(idx2 % 4) * 128 + 256], lhsT=KTv[:, slot, pr, kt * 128:(kt + 1) * 128],
                                    rhs=aqv[:, 2 * pr:2 * pr + 2, c0:c0 + 128], start=True, stop=True),
                                    reads=[f"KT{slot}_{pr}", f"aq{pr}"], writes=[bk])
                                if it % 2 == 1:
                                    nb = idx2 // 4
                                    S.op("act", lambda e, nb=nb, bank=bank: e.activation(out=PT[:, nb * 512:(nb + 1) * 512], in_=pSS3[bank][:, :],
                                                                                         func=AF.Exp, scale=0.125),
                                         reads=[bk], writes=[f"PT{nb}"])
                                    sc_i[0] += 1
                                    yield
                            pk_ = [f"PT{nb}" for nb in range(5 * hg, 5 * hg + 5)]
                            S.op("pool", lambda e, hg=hg: e.memset(PT6[0:64, hg, :, 0, :, 64:128], 0.0), reads=pk_, writes=pk_)
                            for pl in range(2):
                                pr = 2 * hg + pl
                                S.op("dve", lambda e, hg=hg, pl=pl, pr=pr: e.tensor_tensor(
                                    out=PT6[:, hg, pl, 1:3, :, :], in0=PT6[:, hg, pl, 1:3, :, :],
                                    in1=EBMv[:, 2 * pr:2 * pr + 2, :, :].rearrange("p hh j q -> p j hh q"), op=ALU.mult),
                                    reads=pk_ + ["EBM"], writes=pk_)
                            if hg == 1:
                                pv_unit(0)
                            yield
                        yield
                        pv_unit(1)
                        yield
                    else:
                        bb = su["b"]
                        for tk in range(4):
                            ld(stage[:, 0:512], ck[bb, tk * 128:(tk + 1) * 128, :], ["scr_g1", "scr_g2"], rk=["EBS3", "EBSn"])
                            S.op("dve", lambda e: e.tensor_copy(out=PT[:, 0:512], in_=stage[:, 0:512]), reads=["scr_g1", "scr_g2"], writes=PTK)
                            for r in range(4):
                                S.op("pe", lambda e, r=r: e.transpose(pT[:, r * 128:(r + 1) * 128], PT[:, r * 128:(r + 1) * 128], identb[:, :]),
                                     reads=PTK + ["identb"], writes=["pT"])
                            S.op("dve", lambda e, tk=tk: e.tensor_copy(out=KTv[:, 0, :, tk * 128:(tk + 1) * 128], in_=v3(pT[:, 0:512], 4)),
                                 reads=["pT"], writes=[f"KT0_{r}" for r in range(4)])
                            ld(stage[:, 512:1024], cv[bb, tk * 128:(tk + 1) * 128, :], ["scr_g2"], rk=["EBS3", "EBSn"])
                            S.op("act", lambda e, tk=tk: e.activation(out=Vv[:, 0, tk, :, 0:64], in_=v3(stage[:, 512:1024], 8), func=AF.Copy), reads=["scr_g2"],
                                 writes=[f"V0_{tk}"])
                            S.op("pool", lambda e, tk=tk: e.memset(Vv[:, 0, tk, :, 64:65], 1.0), writes=[f"V0_{tk}"])
                            yield
                        for h in range(8):
                            bank = h // 4
                            for tk in range(5):
                                o0 = ((h % 4) * 5 + tk) * 16
                                if tk < 4:
                                    S.op("pe", lambda e, h=h, tk=tk, bank=bank, o0=o0, c0=c0: e.matmul(
                                        pSS[bank][:, o0:o0 + 16], lhsT=KTv[:, 0, h // 2, tk * 128:(tk + 1) * 128],
                                        rhs=aqv[:, h, c0:c0 + 16], start=True, stop=True),
                                        reads=[f"KT0_{h // 2}", f"aq{h // 2}"], writes=["pS" if bank == 0 else "pS2"])
                                else:
                                    S.op("pe", lambda e, h=h, bank=bank, o0=o0, c0=c0: e.matmul(
                                        pSS[bank][0:16, o0:o0 + 16], lhsT=KTv[:, 1, h // 2, c0:c0 + 16],
                                        rhs=aqv[:, h, c0:c0 + 16], start=True, stop=True),
                                        reads=[f"KT1_{h // 2}", f"aq{h // 2}"], writes=["pS" if bank == 0 else "pS2"])
                        PTs = PT[:, 0:640].rearrange("p (h t q) -> p h t q", h=8, t=5)
                        for bank in range(2):
                            S.op("act", lambda e, bank=bank: e.activation(out=v3(PT[:, bank * 320:(bank + 1) * 320], 4)[:, :, 0:64],
                                                                          in_=v3(pSS[bank][:, 0:320], 4)[:, :, 0:64], func=AF.Exp, scale=0.125),
                                 reads=["pS" if bank == 0 else "pS2"], writes=PTK)
                            S.op("act", lambda e, bank=bank: e.activation(out=v3(PT[0:16, bank * 320:(bank + 1) * 320], 4)[:, :, 64:80],
                                                                          in_=v3(pSS[bank][0:16, 0:320], 4)[:, :, 64:80], func=AF.Exp, scale=0.125),
                                 reads=["pS" if bank == 0 else "pS2"], writes=PTK)
                        S.op("pool", lambda e, PTs=PTs: e.tensor_tensor(out=PTs[:, :, 3, :], in0=PTs[:, :, 3, :], in1=v3(EBS3[:], 8), op=ALU.mult),
                             reads=PTK + ["EBS3"], writes=PTK)
                        S.op("pool", lambda e, PTs=PTs: e.tensor_tensor(out=PTs[0:16, :, 4, :], in0=PTs[0:16, :, 4, :], in1=v3(EBSn[:], 8), op=ALU.mult),
                             reads=PTK + ["EBSn"], writes=PTK)
                        yield
                        for h in range(8):
                            px = pX[h // 4]
                            o0 = (h % 4) * 65
                            for tk in range(5):
                                if tk < 4:
                                    S.op("pe", lambda e, h=h, tk=tk, px=px, o0=o0, PTs=PTs: e.matmul(px[0:16, o0:o0 + 65], lhsT=PTs[:, h, tk, :],
                                                                                                     rhs=Vv[:, 0, tk, h, :], start=(tk == 0), stop=False),
                                         reads=PTK + [f"V0_{tk}"], writes=[f"pX{h // 4}"])
                                else:
                                    S.op("pe", lambda e, h=h, px=px, o0=o0, j=j, PTs=PTs: e.matmul(px[0:16, o0:o0 + 65], lhsT=PTs[0:16, h, 4, :],
                                                                                                   rhs=Vv[0:16, 1, j, h, :], start=False, stop=True),
                                         reads=PTK + [f"V1_{j}"], writes=[f"pX{h // 4}"])
                        yield
                    for hb in range(2):
                        S.op("dve", lambda e, hb=hb, n=n: e.reciprocal(out=rden[0:n, hb * 4:hb * 4 + 4].unsqueeze(2),
                                                                       in_=v3(pX[hb][0:n, 0:260], 4)[:, :, 64:65]),
                             reads=[f"pX{hb}"], writes=["rden"])
                    for hb in range(2):
                        S.op("dve", lambda e, hb=hb, n=n: e.tensor_tensor(out=v3(scr_a[0:n, hb * 256:(hb + 1) * 256], 4),
                                                                          in0=v3(pX[hb][0:n, 0:260], 4)[:, :, 0:64],
                                                                          in1=bc(rden[0:n, hb * 4:hb * 4 + 4], [n, 4, 64], 2), op=ALU.mult),
                             reads=[f"pX{hb}", "rden"], writes=["scr_a"])
                    S.op("dve", lambda e, n=n, j=j: e.tensor_tensor(out=ha[0:n, :], in0=scr_a[0:n, :], in1=GAv[0:n, j, :], op=ALU.mult),
                         reads=["scr_a", f"GA{j}"], writes=["ha"])
                    yield
                    yield
                    for r in range(4):
                        S.op("pe", lambda e, r=r, n=n: e.transpose(pT[:, r * 128:r * 128 + n], ha[0:n, r * 128:(r + 1) * 128], identb[0:n, 0:n]),
                             reads=["ha", "identb"], writes=["pT"])
                    S.op("dve", lambda e, n=n, c0=c0: e.tensor_copy(out=haTv[:, :, c0:c0 + n], in_=v3(pT[:, 0:512], 4)[:, :, 0:n]),
                         reads=["pT"], writes=[hak[j]])
                    yield

            def g_g(part="all"):
                NACC = 2 if part == "head" else 3
                for hf in range(2):
                    for which, (nm, boff, toff) in enumerate([(f"gm{hf}", 16, 0), (f"ga{hf}", 24, 4)]):
                        if part == "tail" and hf == 0:
                            continue
                        wt, wk = use_piece(nm)
                        wv_ = v3(wt[:], 8)
                        for d4 in range(4):
                            dc = 4 * hf + d4
                            pb, pk = next_acc(NACC)
                            for kc in range(8):
                                S.op("pe", lambda e, pb=pb, wv_=wv_, kc=kc, d4=d4: e.matmul(pb[:, 0:NT], lhsT=wv_[:, kc, d4 * 128:(d4 + 1) * 128],
                                                                                            rhs=hTv[:, kc, 0:NT], start=(kc == 0), stop=(kc == 7)),
                                     reads=hTk + [wk], writes=[pk])
                            S.op("act", lambda e, pb=pb, bch=boff + dc, ts_=toff + d4: e.activation(out=thgv[:, ts_, 0:NT], in_=pb[:, 0:NT], func=AF.Tanh,
                                                                                                  bias=bfm_h[:, bch:bch + 1], scale=0.5),
                                 reads=[pk, "bfm_h"], writes=[f"thg{toff + d4}"])
                            yield
                    if part == "head":
                        return
                    wbt, wbk = use_piece(f"wb{hf}")
                    wbp = wbt[:].rearrange("p (m c d) -> p m c d", m=2, c=4)
                    for d4 in range(4):
                        dc = 4 * hf + d4
                        for cc in range(4):
                            S.op("pe", lambda e, d4=d4, cc=cc, wbp=wbp: e.matmul(pG[:, 0:NT], lhsT=wbp[:, 0, cc, d4 * 128:(d4 + 1) * 128],
                                                                                 rhs=hmTv[:, cc, 0:NT], start=(cc == 0), stop=(cc == 3)),
                                 reads=hmk + [wbk], writes=["pG"])
                        for cc in range(4):
                            S.op("pe", lambda e, d4=d4, cc=cc, wbp=wbp: e.matmul(pS2[:, 0:NT], lhsT=wbp[:, 1, cc, d4 * 128:(d4 + 1) * 128],
                                                                                 rhs=haTv[:, cc, 0:NT], start=(cc == 0), stop=(cc == 3)),
                                 reads=hak + [wbk], writes=["pS2"])
                        S.op("dve", lambda e, d4=d4: e.scalar_tensor_tensor(out=scr_g1[:, 0:NT], in0=thgv[:, d4, 0:NT], scalar=1.0, in1=pG[:, 0:NT],
                                                                            op0=ALU.add, op1=ALU.mult),
                             reads=[f"thg{d4}", "pG"], writes=["scr_g1"])
                        S.op("dve", lambda e, d4=d4: e.scalar_tensor_tensor(out=scr_g2[:, 0:NT], in0=thgv[:, 4 + d4, 0:NT], scalar=1.0,
                                                                            in1=pS2[:, 0:NT], op0=ALU.add, op1=ALU.mult),
                             reads=[f"thg{4 + d4}", "pS2"], writes=["scr_g2"])
                        S.op("pool", lambda e, dc=dc: e.tensor_tensor(out=mixv[:, dc, 0:NT], in0=scr_g1[:, 0:NT], in1=scr_g2[:, 0:NT], op=ALU.add),
                             reads=["scr_g1", "scr_g2"], writes=[f"mix{dc}"])
                        yield
                yield
                yield
                units = [(eh, j) for eh in range(2) for j in range(nsub)]
                slots = {}

                def reload(u):
                    eh, j = units[u]
                    su = subs[j]
                    n = su["n"]
                    yb = yb_i[0] % 3
                    yb_i[0] += 1
                    slots[u] = yb
                    ld(ybv[0:n, yb, :], xsrc[su["b"], su["t0"]:su["t0"] + n, eh * 512:(eh + 1) * 512], [f"ybuf{yb}"])

                reload(0)
                wt = wk = wov = None
                for u, (eh, j) in enumerate(units):
                    if j == 0:
                        wt, wk = use_piece(f"wout{eh}")
                        wov = v3(wt[:], 8)
                    su = subs[j]
                    n, c0 = su["n"], su["c0"]
                    if u + 1 < len(units):
                        reload(u + 1)
                    yb = slots[u]
                    yk = f"ybuf{yb}"
                    pb, pk = next_acc(3)
                    for dc in range(8):
                        S.op("pe", lambda e, pb=pb, dc=dc, n=n, c0=c0, wov=wov: e.matmul(pb[0:n, :], lhsT=mixv[:, dc, c0:c0 + n],
                                                                                         rhs=wov[:, dc, :], start=(dc == 0), stop=(dc == 7)),
                             reads=[f"mix{dc}", wk], writes=[pk])
                    S.op("dve", lambda e, pb=pb, n=n, yb=yb: e.scalar_tensor_tensor(
                        out=ybv[0:n, yb, :], in0=pb[0:n, :], scalar=0.25, in1=ybv[0:n, yb, :], op0=ALU.mult, op1=ALU.add),
                        reads=[pk, yk], writes=[yk])
                    yo = y_p[su["b"], su["t0"]:su["t0"] + n, eh * 512:(eh + 1) * 512] if kind == "p" else y_s[su["b"], :, eh * 512:(eh + 1) * 512]
                    ld(yo, ybv[0:n, yb, :], [], rk=[yk])
                    yield

            T.xp, T.pl, T.m, T.a, T.g = g_xp, g_pl, g_m, g_a, g_g
            return T

        EARLY = ["mq", "mk", "mv", "mo", "mz"]
        LATE = ["aq", "ak", "av", "az"]
        GP = ["gm0", "ga0", "wb0", "gm1", "ga1", "wb1", "wout0", "wout1"]
        tiles = [make_tile("p", 0, ti, ti) for ti in range(4)]
        tiles.append(make_tile("s", 0, 0, 4))
        tiles += [make_tile("p", 1, ti, 5 + ti) for ti in range(4)]
        NTL = len(tiles)
        piece_seq.extend(EARLY)
        for i in range(NTL):
            piece_seq.extend(LATE)
            if i > 0:
                piece_seq.extend(GP)
            if i + 1 < NTL:
                piece_seq.extend(EARLY)
        piece_seq.extend(GP)

        def chain(*gens):
            for g in gens:
                yield from g

        def count_steps(genfunc_list):
            snap = (acc_i[0], xi_i[0], yb_i[0], acc3_i[0], sc_i[0])
            S.dry = True
            n = 0
            for gf in genfunc_list:
                for _ in gf():
                    n += 1
            S.dry = False
            acc_i[0], xi_i[0], yb_i[0], acc3_i[0], sc_i[0] = snap
            return max(n, 1)

        def interleave(items):
            chains = []
            for gfl in items:
                tot = count_steps(gfl)
                chains.append(dict(gen=chain(*[gf() for gf in gfl]), tot=tot, done=0, alive=True))
            while any(c["alive"] for c in chains):
                c = min((c for c in chains if c["alive"]), key=lambda c: (c["done"] + 1) / c["tot"])
                try:
                    next(c["gen"])
                    c["done"] += 1
                except StopIteration:
                    c["alive"] = False

        interleave([[tiles[0].xp]])
        build_ebm()
        for i, T in enumerate(tiles):
            bulk = [T.pl, tiles[i - 1].g] if i > 0 else [T.pl]
            interleave([[T.m], bulk])
            if i + 1 < NTL:
                interleave([[T.a], [tiles[i + 1].xp]])
            else:
                interleave([[T.a], [lambda T=T: T.g("head")]])
        interleave([[lambda: tiles[-1].g("tail")]])
        assert piece_pos[0] == len(piece_seq), (piece_pos[0], len(piece_seq))
        S.emit()
    return nc


_CACHE = {}


def _host_consts(rel_bias):
    qi = np.arange(128)[None, :]
    ki = np.arange(128)[:, None]
    idx4 = np.clip(qi - ki, -128, 128) + 128
    idx3 = np.clip(qi - ki + 128, -128, 128) + 128
    relb_t = np.stack([rel_bias[:, idx3], rel_bias[:, idx4]], axis=1)
    relb_t = np.ascontiguousarray(relb_t.transpose(2, 0, 1, 3)).reshape(128, 8 * 2 * 128)
    relb_c = np.ascontiguousarray(np.broadcast_to(rel_bias[None, :, 256], (128, 8)))
    q16 = np.arange(16)[None, :]
    idx_s3 = np.clip(q16 + 128 - ki, -128, 128) + 128
    relb_s3 = np.ascontiguousarray(rel_bias[:, idx_s3].transpose(1, 0, 2)).reshape(128, 8 * 16)
    k16 = np.arange(16)[:, None]
    idx_sn = np.clip(q16 - k16, -128, 128) + 128
    relb_sn = np.ascontiguousarray(rel_bias[:, idx_sn].transpose(1, 0, 2)).reshape(16, 8 * 16)
    return relb_t, relb_c, relb_s3, relb_sn


def kernel(x_prompt, x_sample, state_mlstm_C, state_mlstm_n, state_mlstm_m, state_mlstm_conv,
           cache_attn_k, cache_attn_v, norm_g, w_in, b_in, conv_w, conv_b, m_head_g,
           q_norm_g, k_norm_g, rel_bias, w_bm, w_ba, w_out):
    f = lambda a: np.ascontiguousarray(np.asarray(a, dtype=np.float32))
    x_prompt, x_sample = f(x_prompt), f(x_sample)
    w_in0, b_in0 = f(w_in)[0], f(b_in)[0]
    wbm0, wba0, wout0 = f(w_bm)[0], f(w_ba)[0], f(w_out)[0]

    def kmajor(w, nk):
        return w.reshape(nk, 128, w.shape[1]).transpose(1, 0, 2)

    pieces = []
    for name in PIECES:
        if name in W_COL:
            c = W_COL[name]
            pieces.append(kmajor(w_in0[:, c:c + 512], 8).reshape(128, 4096))
        elif name.startswith("wb"):
            hf = int(name[2])
            a = kmajor(wbm0[:, hf * 512:(hf + 1) * 512], 4).reshape(128, 2048)
            bb = kmajor(wba0[:, hf * 512:(hf + 1) * 512], 4).reshape(128, 2048)
            pieces.append(np.concatenate([a, bb], axis=1))
        else:
            eh = int(name[4])
            pieces.append(kmajor(wout0[:, eh * 512:(eh + 1) * 512], 8).reshape(128, 4096))
    w_all = np.stack(pieces)
    w_g = np.ascontiguousarray(kmajor(w_in0[:, 2560:2568], 8)).reshape(128, 64)
    fm_cols = [0, 512, 2568, 3080, 4616, 5128, 5640, 6152]
    b_fm = np.concatenate([b_in0[c:c + 512].reshape(4, 128).T for c in fm_cols], axis=1)
    b_g = np.stack([b_in0[2560:2564], b_in0[2564:2568]], axis=1)
    b_tm = np.stack([b_in0[c:c + 512] for c in (1024, 1536, 2048, 3592, 4104)])
    sel5 = np.repeat(np.eye(5, dtype=np.float32), 128, axis=1)
    gx = f(norm_g)[0].reshape(8, 128).T
    cw = f(conv_w)[0].reshape(4, 8, 128).transpose(2, 1, 0).reshape(128, 32)
    cb = f(conv_b)[0][None, :]
    mg = f(m_head_g)[0].T
    gqk = np.stack([np.tile(f(q_norm_g)[0], 2), np.tile(f(k_norm_g)[0], 2)], axis=1)
    relb_t, relb_c, relb_s3, relb_sn = _host_consts(f(rel_bias)[0])
    ident = np.eye(128, dtype=np.float32)
    ki = np.arange(128)[:, None]; qi = np.arange(128)[None, :]
    cmask = (ki <= qi).astype(np.float32)
    blk = ((ki // 64) == (qi // 64)).astype(np.float32)
    vm0 = 1.0 - ((ki < 64) & (qi >= 64)).astype(np.float32)
    vm3 = np.ones((128, 128), np.float32)
    vm4 = 1.0 - ((ki >= 64) & (qi < 64)).astype(np.float32)
    vmask = np.stack([vm0, vm3, vm4], axis=1).reshape(128, 384)
    shared = dict(w_all=w_all, w_g=w_g, b_fm=b_fm, b_g=b_g, b_tm=b_tm, sel5=sel5, gx=gx, cw=cw, cb=cb, mg=mg, gqk=gqk,
                  relb_t=relb_t, relb_c=relb_c, relb_s3=relb_s3, relb_sn=relb_sn,
                  ident=ident, cmask=cmask, blk64=blk, vmask=vmask)
    shared = {k: np.ascontiguousarray(v, dtype=np.float32) for k, v in shared.items()}
    sC, sn, sm, sconv = f(state_mlstm_C)[0], f(state_mlstm_n)[0], f(state_mlstm_m)[0], f(state_mlstm_conv)[0]
    ckf, cvf = f(cache_attn_k)[0].reshape(16, 512, 512), f(cache_attn_v)[0].reshape(16, 512, 512)
    in_maps = []
    for c in range(NCORES):
        sl = slice(2 * c, 2 * c + 2)
        m = dict(shared)
        m.update(x_p=x_prompt[sl], x_s=x_sample[sl], st_C=sC[sl], st_n=sn[sl], st_m=sm[sl], st_conv=sconv[sl],
                 ck=ckf[sl], cv=cvf[sl])
        in_maps.append({k: np.ascontiguousarray(v) for k, v in m.items()})
    if "nc" not in _CACHE:
        _CACHE["nc"] = build_program()
    res = run_bass_kernel_spmd(_CACHE["nc"], in_maps, core_ids=list(range(NCORES)))
    R = res.results
    cat = lambda k: np.concatenate([np.asarray(r[k], dtype=np.float32) for r in R], axis=0)
    return (cat("y_p"), cat("y_s"),
            cat("C_p")[None], cat("n_p")[None], cat("m_p")[None], cat("conv_p")[None],
            cat("k_p").reshape(16, 512, 8, 64)[None], cat("v_p").reshape(16, 512, 8, 64)[None],
            cat("C_s")[None], cat("n_s")[None], cat("m_s")[None], cat("conv_s")[None],
            cat("k_s").reshape(16, TS, 8, 64)[None], cat("v_s").reshape(16, TS, 8, 64)[None])
```

```python
import contextlib
import math
import numpy as np
import concourse.bass as bass
import concourse.mybir as mybir
from concourse.bass_utils import run_bass_kernel_spmd

F32 = mybir.dt.float32
BF16 = mybir.dt.bfloat16
AF = mybir.ActivationFunctionType
ALU = mybir.AluOpType

NCORES = 8
D = 1024
SEQ = 2048
TS = 16
EPS = 1e-6
EPOCH = 12000
NDMASEM = 3
LNCK = math.log(128.0 ** -0.5)

COLS = dict(mq=0, mk=512, mv=1024, mo=1536, mz=2048, mi=2560, mf=2564, aq=2568, ak=3080,
            av=3592, az=4104, gm=4616, ga=5640)
STREAM = [("mq", 0), ("mk", 512), ("mv", 1024), ("mo", 1536), ("mz", 2048),
          ("aq", 2568), ("ak", 3080), ("av", 3592), ("az", 4104),
          ("gm0", 4616), ("gm1", 5128), ("ga0", 5640), ("ga1", 6152)]
FM_CH = dict(mq=0, mk=4, aq=8, ak=12, gm0=16, gm1=20, ga0=24, ga1=28)
TM_IDX = dict(mv=0, mo=1, mz=2, av=3, az=4)
NWB = 3
PSUM_KEYS = frozenset(['pA', 'pB', 'pT', 'pS', 'pS2', 'pX0', 'pX1', 'pG'])
import os as _os
PROBE_NOLOAD = _os.environ.get('KPROBE', '') == 'noload'


class Sched:
    ENGS = ("pe", "act", "dve", "pool", "sp")

    def __init__(self, nc):
        self.nc = nc
        self.ops = []
        self.per_eng = {e: [] for e in self.ENGS}
        self.last_w = {}
        self.readers = {}
        self.dma_count = {e: 0 for e in self.ENGS}

    dry = False

    def op(self, eng, fn, reads=(), writes=(), dma=False):
        if self.dry:
            return -1
        oid = len(self.ops)
        deps = set()
        for k in reads:
            w = self.last_w.get(k)
            if w is not None:
                deps.add(w)
            if k in PSUM_KEYS:
                for r in self.readers.get(k, ()):
                    if self.ops[r]["eng"] != eng:
                        deps.add(r)
        for k in writes:
            w = self.last_w.get(k)
            if w is not None:
                deps.add(w)
            for r in self.readers.get(k, ()):
                deps.add(r)
        rec = dict(eng=eng, fn=fn, deps=deps, dma=dma, idx=len(self.per_eng[eng]), id=oid)
        if dma:
            rec["dma_i"] = self.dma_count[eng]
            self.dma_count[eng] += 1
        self.ops.append(rec)
        self.per_eng[eng].append(rec)
        for k in reads:
            self.readers.setdefault(k, []).append(oid)
        for k in writes:
            self.last_w[k] = oid
            self.readers[k] = []
        return oid

    def emit(self, final_wait_eng="sp"):
        nc = self.nc
        ops = self.ops
        waited = {e: {} for e in self.ENGS}
        waited_dma = {e: set() for e in self.ENGS}
        for rec in ops:
            ce = rec["eng"]
            need = {}
            need_dma = []
            for d in rec["deps"]:
                p = ops[d]
                if p["dma"]:
                    if d not in waited_dma[ce]:
                        need_dma.append(d)
                else:
                    pe_ = p["eng"]
                    if pe_ == "pe" and ce == "pe":
                        continue
                    if p["idx"] > waited[ce].get(pe_, -1):
                        need[pe_] = max(need.get(pe_, -1), p["idx"])
            for pe_, ix in need.items():
                waited[ce][pe_] = ix
            for d in need_dma:
                waited_dma[ce].add(d)
            rec["w_eng"] = need
            rec["w_dma"] = sorted(need_dma)
            rec["sig"] = False
        for rec in ops:
            for pe_, ix in rec["w_eng"].items():
                self.per_eng[pe_][ix]["sig"] = True
        for e in self.ENGS:
            comp = [r for r in self.per_eng[e] if not r["dma"]]
            if comp:
                comp[-1]["sig"] = True
        nsig = {}
        for e in self.ENGS:
            c = 0
            for r in self.per_eng[e]:
                if (not r["dma"]) and r["sig"]:
                    r["sig_n"] = c
                    c += 1
            nsig[e] = c
        stack = contextlib.ExitStack()
        sems = {}
        for e in self.ENGS:
            nep = (nsig[e] + EPOCH - 1) // EPOCH
            sems[e] = [stack.enter_context(nc.semaphore(f"s_{e}_{i}")) for i in range(nep)]
        dsems = {}
        for e in self.ENGS:
            if self.dma_count[e]:
                dsems[e] = [stack.enter_context(nc.semaphore(f"d_{e}_{i}")) for i in range(NDMASEM)]

        def sem_for(e, n):
            return sems[e][n // EPOCH], (n % EPOCH) + 1

        def dma_sem_for(rec):
            i = rec["dma_i"]
            return dsems[rec["eng"]][i % NDMASEM], 16 * (i // NDMASEM + 1)

        def run_engine(e, eng):
            for rec in self.per_eng[e]:
                for pe_, ix in rec["w_eng"].items():
                    s, v = sem_for(pe_, self.per_eng[pe_][ix]["sig_n"])
                    eng.wait_ge(s, v)
                for d in rec["w_dma"]:
                    s, v = dma_sem_for(ops[d])
                    eng.wait_ge(s, v)
                if rec["dma"]:
                    s, v = dma_sem_for(rec)
                    if v > 16:
                        eng.wait_ge(s, v - 16)
                    rec["fn"](eng).then_inc(s, 16)
                else:
                    ins = rec["fn"](eng)
                    if rec["sig"]:
                        s, v = sem_for(e, rec["sig_n"])
                        ins.then_inc(s, 1)
            if e == final_wait_eng:
                for e2 in self.ENGS:
                    n = self.dma_count[e2]
                    for j in range(min(n, NDMASEM)):
                        cnt = (n - 1 - j) // NDMASEM + 1
                        eng.wait_ge(dsems[e2][j], 16 * cnt)
                    if nsig[e2]:
                        s, v = sem_for(e2, nsig[e2] - 1)
                        eng.wait_ge(s, v)

        with stack:
            with nc.Block() as block:
                @block.tensor
                def _(eng):
                    run_engine("pe", eng)

                @block.scalar
                def _(eng):
                    run_engine("act", eng)

                @block.vector
                def _(eng):
                    run_engine("dve", eng)

                @block.gpsimd
                def _(eng):
                    run_engine("pool", eng)

                @block.sync
                def _(eng):
                    run_engine("sp", eng)


PIECES = ["mq", "mk", "mv", "mo", "mz", "aq", "ak", "av", "az",
          "gm0", "ga0", "wb0", "gm1", "ga1", "wb1", "wout0", "wout1"]
PIDX = {n: i for i, n in enumerate(PIECES)}
W_COL = dict(mq=0, mk=512, mv=1024, mo=1536, mz=2048, aq=2568, ak=3080, av=3592, az=4104,
             gm0=4616, gm1=5128, ga0=5640, ga1=6152)
FMC = dict(mq=0, mk=4, aq=8, ak=12)


def build_program():
    nc = bass.Bass("TRN2", target_bir_lowering=False)
    S = Sched(nc)

    def din(name, shape, dt=F32):
        return nc.dram_tensor(name, list(shape), dt, kind="ExternalInput").ap()

    def dout(name, shape, dt=F32):
        return nc.dram_tensor(name, list(shape), dt, kind="ExternalOutput").ap()

    x_p = din("x_p", [2, SEQ, D]); x_s = din("x_s", [2, TS, D])
    st_C = din("st_C", [2, 4, 128, 128]); st_n = din("st_n", [2, 4, 128]); st_m = din("st_m", [2, 4])
    st_conv = din("st_conv", [2, 3, D]); ck = din("ck", [2, 512, 512]); cv = din("cv", [2, 512, 512])
    w_all = din("w_all", [17, 128, 4096]); w_g = din("w_g", [128, 64])
    b_fm = din("b_fm", [128, 32]); b_g = din("b_g", [4, 2]); b_tm = din("b_tm", [5, 512]); sel5_d = din("sel5", [5, 5 * 128])
    gx_d = din("gx", [128, 8]); cw_d = din("cw", [128, 32]); cb_d = din("cb", [1, D])
    mg_d = din("mg", [128, 4]); gqk_d = din("gqk", [128, 2])
    relb_t = din("relb_t", [128, 8 * 2 * 128]); relb_c = din("relb_c", [128, 8])
    relb_s3 = din("relb_s3", [128, 8 * 16]); relb_sn = din("relb_sn", [16, 8 * 16])
    ident_d = din("ident", [128, 128]); cmask_d = din("cmask", [128, 128]); blk_d = din("blk64", [128, 128])
    vmask_d = din("vmask", [128, 3 * 128])
    wscr = nc.dram_tensor("wscr", [17, 128, 4096], BF16, kind="Internal").ap()

    y_p = dout("y_p", [2, SEQ, D]); y_s = dout("y_s", [2, TS, D])
    C_p = dout("C_p", [2, 4, 128, 128]); n_p = dout("n_p", [2, 4, 128]); m_p = dout("m_p", [2, 4])
    conv_p = dout("conv_p", [2, 3, D]); k_p = dout("k_p", [2, 512, 512]); v_p = dout("v_p", [2, 512, 512])
    C_s = dout("C_s", [2, 4, 128, 128]); n_s = dout("n_s", [2, 4, 128]); m_s = dout("m_s", [2, 4])
    conv_s = dout("conv_s", [2, 3, D]); k_s = dout("k_s", [2, TS, 512]); v_s = dout("v_s", [2, TS, 512])

    st = contextlib.ExitStack()

    def sb(name, shape, dt):
        return st.enter_context(nc.sbuf_tensor("sb_" + name, list(shape), dt))

    def ps(name, shape, dt):
        return st.enter_context(nc.psum_tensor("ps_" + name, list(shape), dt))

    with st:
        ident_f = sb("ident_f", [128, 128], F32)
        identb = sb("identb", [128, 128], BF16)
        cmask = sb("cmask", [128, 128], BF16)
        blk64 = sb("blk64", [128, 128], BF16)
        half_row = sb("half_row", [1, 512], BF16)
        bfm = sb("bfm", [128, 32], F32)
        bfm_h = sb("bfm_h", [128, 32], F32)
        bg = sb("bg", [4, 2], F32)
        nbg = sb("nbg", [4, 2], F32)
        btm = sb("btm", [5, 512], BF16)
        sel5 = sb("sel5", [5, 5 * 128], BF16)
        gx = sb("gx", [128, 8], F32)
        cw = sb("cw", [128, 32], F32)
        cbrow = sb("cbrow", [1, D], BF16)
        mg = sb("mg", [128, 4], F32)
        gqk = sb("gqk", [128, 2], F32)
        diagW = sb("diagW", [128, 32 * 128], BF16)
        wg = sb("wg", [128, 64], BF16)
        EBM = sb("EBM", [128, 8 * 2 * 128], BF16)
        EBS3 = sb("EBS3", [128, 8 * 16], BF16)
        EBSn = sb("EBSn", [16, 8 * 16], BF16)
        relc = sb("relc", [128, 8], F32)
        vmk = sb("vmk", [128, 384], BF16)
        wbuf = [sb(f"wbuf{i}", [128, 4096], BF16) for i in range(NWB)]
        xin = sb("xin", [128, 2 * D], F32)
        ybuf = sb("ybuf", [128, 3 * 512], F32)
        xs = sb("xs", [128, D], BF16)
        ssx = sb("ssx", [128, 4], F32)
        rsx = sb("rsx", [128, 4], F32)
        hT2 = [sb(f"hT{i}", [128, 8 * 512], BF16) for i in range(2)]
        zbuf = sb("zbuf", [128, 8 * 520], BF16)
        convo = sb("convo", [128, 2 * 8 * 3], F32)
        qkT = sb("qkT", [128, 8 * 512], BF16)
        tht = sb("tht", [128, 512], BF16)
        vaug = sb("vaug", [128, 4 * 4 * 129], BF16)
        vu = sb("vu", [128, 4 * 129], BF16)
        tho = sb("tho", [128, 4 * 512], BF16)
        thz = sb("thz", [128, 512], BF16)
        scr_mz = sb("scr_mz", [128, 512], F32)
        scr_a = sb("scr_a", [128, 512], F32)
        stage = sb("stage", [128, 1024], F32)
        scr_g1 = stage[:, 0:512]
        scr_g2 = stage[:, 512:1024]
        GM = sb("GM", [128, 4 * 512], BF16)
        GA = sb("GA", [128, 4 * 512], BF16)
        thg = sb("thg", [128, 8 * 512], BF16)
        qraw2 = [sb(f"qraw{i}", [128, 512], BF16) for i in range(2)]
        qsq2 = [sb(f"qsq{i}", [128, 512], BF16) for i in range(2)]
        rstd = sb("rstd", [128, 512], F32)
        aqT = sb("aqT", [128, 8 * 512], BF16)
        KTr = sb("KTr", [128, 2 * 4 * 512], BF16)
        Vr = sb("Vr", [128, 2 * 4 * 8 * 65], BF16)
        PT = sb("PT", [128, 40 * 128], BF16)
        ha = sb("ha", [128, 512], BF16)
        haT = sb("haT", [128, 4 * 512], BF16)
        hm = sb("hm", [128, 512], BF16)
        hmT2 = [sb(f"hmT{i}", [128, 4 * 512], BF16) for i in range(2)]
        ktok = sb("ktok", [128, 512], BF16)
        SW = sb("SW", [128, 512], BF16)
        C_f = sb("C_f", [128, 4 * 129], F32)
        Cdec_b = sb("Cdec_b", [128, 4 * 129], BF16)
        mixT = sb("mixT", [128, 8 * 512], BF16)
        R1 = sb("R1", [4, 512], F32)
        R2 = sb("R2", [4, 512], F32)
        R3 = sb("R3", [4, 512], F32)
        ones4 = sb("ones4", [4, 128], F32)
        Bext = sb("Bext", [4, 544], F32)
        Gext = sb("Gext", [4, 544], F32)
        darg = sb("darg", [4, 4], F32)
        dec = sb("dec", [4, 4], F32)
        decd = sb("decd", [4, 32], F32)
        mfin = sb("mfin", [4, 1], F32)
        gcols = sb("gcols", [128, 4 * 16], F32)
        dcol = sb("dcol", [128, 4], F32)
        rcol = sb("rcol", [128, 4], F32)
        sscol = sb("sscol", [128, 4], F32)
        tcol = sb("tcol", [128, 4], F32)
        t2col = sb("t2col", [128, 4], F32)
        rho = sb("rho", [128, 4], F32)
        rden = sb("rden", [128, 8], F32)
        pA = ps("pA", [128, 512], F32); pB = ps("pB", [128, 512], F32)
        pT = ps("pT", [128, 1024], BF16)
        pS = ps("pS", [128, 512], F32); pS2 = ps("pS2", [128, 512], F32)
        pX0 = ps("pX0", [128, 512], F32); pX1 = ps("pX1", [128, 512], F32)
        pG = ps("pG", [128, 512], F32)
        pX = [pX0, pX1]
        pSS = [pS, pS2]
        pSS3 = [pS, pS2, pG]
        sc_i = [0]
        acc = [(pA, "pA"), (pB, "pB")]
        acc_i = [0]

        acc3 = acc + [(pS2, "pS2")]
        acc3_i = [0]

        def next_acc(n=2):
            if n == 3:
                r = acc3[acc3_i[0] % 3]
                acc3_i[0] += 1
                return r
            r = acc[acc_i[0] % 2]
            acc_i[0] += 1
            return r

        def v3(ap, a):
            return ap.rearrange("p (a b) -> p a b", a=a)

        def bc(ap, shape, axis):
            return ap.unsqueeze(axis).to_broadcast(list(shape))

        def ld(dst, src, wk, eng="sp", rk=(), **kw):
            S.op(eng, lambda e, d=dst, s=src, kw=kw: e.dma_start(out=d, in_=s, **kw), reads=list(rk), writes=list(wk), dma=True)

        ld(ident_f[:], ident_d, ["ident_f"])
        ld(stage[:, 0:128], cmask_d, ["st_a"])
        ld(stage[:, 128:256], blk_d, ["st_b"])
        ld(stage[:, 256:640], vmask_d, ["st_c"])
        ld(bfm[:], b_fm, ["bfm"]); ld(bg[:], b_g, ["bg"])
        ld(gx[:], gx_d, ["gx"]); ld(cw[:], cw_d, ["cw"])
        ld(mg[:], mg_d, ["mg"]); ld(gqk[:], gqk_d, ["gqk"]); ld(relc[:], relb_c, ["relc"])
        S.op("dve", lambda e: e.tensor_copy(out=identb[:], in_=ident_f[:]), reads=["ident_f"], writes=["identb"])
        S.op("dve", lambda e: e.tensor_copy(out=cmask[:], in_=stage[:, 0:128]), reads=["st_a"], writes=["cmask"])
        S.op("dve", lambda e: e.tensor_copy(out=blk64[:], in_=stage[:, 128:256]), reads=["st_b"], writes=["blk64"])
        S.op("dve", lambda e: e.tensor_copy(out=vmk[:], in_=stage[:, 256:640]), reads=["st_c"], writes=["vmk"])
        S.op("pool", lambda e: e.memset(half_row[:], 0.5), writes=["half_row"])
        S.op("pool", lambda e: e.memset(ones4[:], 1.0), writes=["ones4"])
        S.op("pool", lambda e: e.memset(aqT[:], 0.0), writes=[f"aq{r}" for r in range(4)])
        S.op("pool", lambda e: e.memset(v3(vaug[:], 16)[:, :, 128:129], 1.0), writes=[f"vaug{j}" for j in range(4)])
        S.op("dve", lambda e: e.tensor_scalar(out=bfm_h[:], in0=bfm[:], scalar1=0.5, scalar2=None, op0=ALU.mult),
             reads=["bfm"], writes=["bfm_h"])
        S.op("dve", lambda e: e.tensor_scalar(out=nbg[:], in0=bg[:], scalar1=-1.0, scalar2=None, op0=ALU.mult),
             reads=["bg"], writes=["nbg"])
        S.op("dve", lambda e: e.tensor_scalar(out=cw[:], in0=cw[:], scalar1=0.5, scalar2=None, op0=ALU.mult),
             reads=["cw"], writes=["cw"])
        for i in range(32):
            S.op("pool" if i % 2 else "dve",
                 lambda e, i=i: e.tensor_scalar(out=diagW[:, i * 128:(i + 1) * 128], in0=ident_f[:], scalar1=cw[:, i:i + 1],
                                                scalar2=None, op0=ALU.mult),
                 reads=["ident_f", "cw"], writes=[f"diagW{i}"])
        S.op("dve", lambda e: e.tensor_scalar(out=relc[:], in0=relc[:], scalar1=-1.0, scalar2=None, op0=ALU.mult),
             reads=["relc"], writes=["relc"])
        ld(wg[:], w_g, ["wg", "castchain"], eng="pool")
        ld(btm[:], b_tm, ["btm", "castchain"], eng="pool")
        ld(sel5[:], sel5_d, ["sel5", "castchain"], eng="pool")
        ld(cbrow[:], cb_d, ["cbrow", "castchain"], eng="pool")
        EBMv = EBM[:].rearrange("p (h j q) -> p h j q", h=8, j=2)
        vm = v3(vmk[:], 3)
        def build_ebm():
            for hh in range(2):
                ld(stage[:, :], relb_t[:, hh * 1024:(hh + 1) * 1024], ["scr_g1", "scr_g2"], rk=["cmask", "blk64", "vmk"])
                for h4 in range(4):
                    h = hh * 4 + h4
                    S.op("act", lambda e, h=h, h4=h4: e.activation(out=stage[:, h4 * 256:(h4 + 1) * 256], in_=stage[:, h4 * 256:(h4 + 1) * 256],
                                                                   func=AF.Exp, bias=relc[:, h:h + 1], scale=1.0),
                         reads=["scr_g1", "scr_g2", "relc"], writes=["scr_g1", "scr_g2"])
                    S.op("dve", lambda e, h=h, h4=h4: e.tensor_tensor(out=EBMv[:, h, :, :], in0=v3(stage[:, h4 * 256:(h4 + 1) * 256], 2),
                                                                      in1=vm[:, 1:3, :], op=ALU.mult),
                         reads=["scr_g1", "scr_g2", "vmk"], writes=["EBM"])
            ld(stage[:, 0:128], relb_s3, ["scr_g1", "scr_g2"])
            ld(stage[0:16, 128:256], relb_sn, ["scr_g1", "scr_g2"])
            for h in range(8):
                S.op("act", lambda e, h=h: e.activation(out=EBS3[:, h * 16:(h + 1) * 16], in_=stage[:, h * 16:(h + 1) * 16],
                                                        func=AF.Exp, bias=relc[:, h:h + 1], scale=1.0),
                     reads=["scr_g1", "scr_g2", "relc"], writes=["EBS3"])
                S.op("act", lambda e, h=h: e.activation(out=EBSn[:, h * 16:(h + 1) * 16], in_=stage[0:16, 128 + h * 16:128 + (h + 1) * 16],
                                                        func=AF.Exp, bias=relc[0:16, h:h + 1], scale=1.0),
                     reads=["scr_g1", "scr_g2", "relc"], writes=["EBSn"])

        zv = v3(zbuf[:], 8); qkv = v3(qkT[:], 8)
        hTv2 = [v3(t[:], 8) for t in hT2]
        hmTv2 = [v3(t[:], 4) for t in hmT2]
        vaugv = vaug[:].rearrange("p (j h c) -> p j h c", j=4, h=4)
        vuv = v3(vu[:], 4)
        GMv = v3(GM[:], 4); GAv = v3(GA[:], 4); thov = v3(tho[:], 4); thgv = v3(thg[:], 8)
        aqv = v3(aqT[:], 8)
        KTv = KTr[:].rearrange("p (s r t) -> p s r t", s=2, r=4)
        Vv = Vr[:].rearrange("p (s j h c) -> p s j h c", s=2, j=4, h=8)
        PTv = PT[:].rearrange("p (h j q) -> p h j q", h=8, j=5)
        haTv = v3(haT[:], 4)
        Cfv = v3(C_f[:], 4); Cdbv = v3(Cdec_b[:], 4)
        mixv = v3(mixT[:], 8)
        wgv = v3(wg[:], 8)
        diagv = diagW[:].rearrange("p (c j q) -> p c j q", c=8, j=4)
        gcv = v3(gcols[:], 4)
        xinv = v3(xin[:], 2); ybv = v3(ybuf[:], 3)
        convov = convo[:].rearrange("p (g c r) -> p g c r", g=2, c=8)
        goff = [0, 520]
        xi_i = [0]
        yb_i = [0]
        PTK = [f"PT{nb}" for nb in range(10)]

        piece_seq = []
        piece_pos = [0]
        piece_loaded = [0]

        cast_done = set()

        def ensure_cast(name):
            if name not in cast_done:
                cast_done.add(name)
                g = PIDX[name]
                ld(wscr[g], w_all[g], [f"wscr{g}", "castchain"], eng="pool")

        def use_piece(name):
            if S.dry:
                return wbuf[0], "wbuf0"
            k = piece_pos[0]
            assert piece_seq[k] == name, (k, piece_seq[k], name)
            piece_pos[0] += 1
            while piece_loaded[0] < min(len(piece_seq), k + NWB):
                q = piece_loaded[0]
                for qq in range(q, min(len(piece_seq), q + 3)):
                    ensure_cast(piece_seq[qq])
                slot = q % NWB
                if not (PROBE_NOLOAD and q >= NWB):
                    ld(wbuf[slot][:], wscr[PIDX[piece_seq[q]]], [f"wbuf{slot}"], rk=[f"wscr{PIDX[piece_seq[q]]}"])
                piece_loaded[0] += 1
            slot = k % NWB
            return wbuf[slot], f"wbuf{slot}"

        class Tile:
            pass

        def make_tile(kind, b, ti, idx):
            T = Tile()
            if kind == "p":
                subs = [dict(b=b, t0=512 * ti + 128 * j, n=128, c0=128 * j) for j in range(4)]
                groups = [dict(b=b, c0=0, n=512, zoff=0, subs=[0, 1, 2, 3], gi=0)]
                NT = 512
                first = ti == 0
                last = ti == 3
                xsrc = x_p
            else:
                subs = [dict(b=bb, t0=0, n=TS, c0=TS * bb) for bb in range(2)]
                groups = [dict(b=bb, c0=TS * bb, n=TS, zoff=19 * bb, subs=[bb], gi=bb) for bb in range(2)]
                NT = 2 * TS
                first = True
                last = True
                xsrc = x_s
            nsub = len(subs)
            par = idx % 2
            hTv = hTv2[par]
            hmTv = hmTv2[par]
            hTk = [f"hT{par}_{j}" for j in range(nsub)]
            hmk = [f"hmT{par}_{j}" for j in range(nsub)]
            hak = [f"haT{j}" for j in range(nsub)]
            cur_slot = ti % 2 if kind == "p" else 1

            def g_xp():
                if first:
                    for g in groups:
                        gi = g["gi"]
                        o = goff[gi]
                        S.op("pool", lambda e, o=o: e.memset(Bext[:, o:o + 1], 0.0), writes=[f"Bext{gi}"])
                        if kind == "p":
                            S.op("pool", lambda e, o=o: e.memset(Gext[:, o:o + 1], 0.0), writes=[f"Gext{gi}"])
                            S.op("pool", lambda e: e.memset(zv[:, :, 0:3], 0.0), writes=[f"z{c}" for c in range(8)])
                        else:
                            ld(Gext[:, o:o + 1], st_m[g["b"]].rearrange("(h o) -> h o", o=1), [f"Gext{gi}"])
                            ld(stage[0:3, :], st_conv[g["b"]], ["scr_g1", "scr_g2"], rk=["EBS3", "EBSn"])
                            for ch in range(8):
                                S.op("pe", lambda e, ch=ch: e.matmul(pG[:, ch * 3:(ch + 1) * 3], lhsT=stage[0:3, ch * 128:(ch + 1) * 128],
                                                                     rhs=ident_f[0:3, 0:3], start=True, stop=True),
                                     reads=["scr_g1", "scr_g2", "ident_f"], writes=["pG"])
                            S.op("dve", lambda e, zo=g["zoff"]: e.tensor_copy(out=zv[:, :, zo:zo + 3], in_=v3(pG[:, 0:24], 8)),
                                 reads=["pG"], writes=[f"z{c}" for c in range(8)])
                    if kind == "p":
                        S.op("pool", lambda e: e.memset(C_f[:], 0.0), writes=["C_f"])
                    yield
                for j, su in enumerate(subs):
                    n = su["n"]
                    xb = xi_i[0] % 2
                    xi_i[0] += 1
                    xk = f"xin{xb}"
                    ld(xinv[0:n, xb, :], xsrc[su["b"], su["t0"]:su["t0"] + n, :], [xk])
                    S.op("act", lambda e, j=j, n=n, xb=xb: e.activation(out=xs[0:n, :], in_=xinv[0:n, xb, :], func=AF.Square,
                                                                        accum_out=ssx[0:n, j:j + 1]),
                         reads=[xk], writes=["xs", f"ssx{j}"])
                    S.op("act", lambda e, j=j, n=n: e.activation(out=rsx[0:n, j:j + 1], in_=ssx[0:n, j:j + 1], func=AF.Ln,
                                                                 bias=EPS, scale=1.0 / D),
                         reads=[f"ssx{j}"], writes=[f"rsx{j}"])
                    S.op("act", lambda e, j=j, n=n: e.activation(out=rsx[0:n, j:j + 1], in_=rsx[0:n, j:j + 1], func=AF.Exp, scale=-0.5),
                         reads=[f"rsx{j}"], writes=[f"rsx{j}"])
                    S.op("dve", lambda e, j=j, n=n, xb=xb: e.tensor_scalar(out=xs[0:n, :], in0=xinv[0:n, xb, :], scalar1=rsx[0:n, j:j + 1],
                                                                           scalar2=None, op0=ALU.mult),
                         reads=[xk, f"rsx{j}"], writes=["xs"])
                    yield
                    yield
                    for kc in range(8):
                        S.op("pe", lambda e, kc=kc, n=n: e.transpose(pT[:, kc * 128:kc * 128 + n], xs[0:n, kc * 128:(kc + 1) * 128],
                                                                     identb[0:n, 0:n]),
                             reads=["xs", "identb"], writes=["pT"])
                    S.op("dve", lambda e, su=su, n=n: e.tensor_tensor(out=hTv[:, :, su["c0"]:su["c0"] + n],
                                                                      in0=v3(pT[:], 8)[:, :, 0:n],
                                                                      in1=bc(gx[:], [128, 8, n], 2), op=ALU.mult),
                         reads=["pT", "gx"], writes=[hTk[j]])
                    yield
                deferred = []
                for gsel, (pb, pk) in enumerate([(pA, "pA"), (pB, "pB")]):
                    for kc in range(8):
                        S.op("pe", lambda e, kc=kc, gsel=gsel, pb=pb: e.matmul(pb[0:4, 0:NT], lhsT=wgv[:, kc, gsel * 4:gsel * 4 + 4],
                                                                               rhs=hTv[:, kc, 0:NT], start=(kc == 0), stop=(kc == 7)),
                             reads=hTk + ["wg"], writes=[pk])
                acc_i[0] = 0
                S.op("act", lambda e: e.activation(out=R1[:, 0:NT], in_=pA[0:4, 0:NT], func=AF.Identity, bias=bg[:, 0:1], scale=1.0),
                     reads=["pA", "bg"], writes=["R1"])
                S.op("act", lambda e: e.activation(out=R2[:, 0:NT], in_=pB[0:4, 0:NT], func=AF.Exp, bias=nbg[:, 1:2], scale=-1.0),
                     reads=["pB", "nbg"], writes=["R2"])
                S.op("act", lambda e: e.activation(out=R2[:, 0:NT], in_=R2[:, 0:NT], func=AF.Ln, bias=1.0, scale=1.0),
                     reads=["R2"], writes=["R2"])
                yield
                for g in groups:
                    gi, c0, n = g["gi"], g["c0"], g["n"]
                    o = goff[gi]
                    L = subs[g["subs"][0]]["n"]
                    nch = n // L
                    Bk, Gk = f"Bext{gi}", f"Gext{gi}"
                    Bn = Bext[:, o + 1:o + 1 + n]
                    Gn = Gext[:, o + 1:o + 1 + n]
                    S.op("dve", lambda e, Bn=Bn, c0=c0, n=n, o=o: e.tensor_tensor_scan(out=Bn, data0=ones4[:, 0:1].to_broadcast([4, n]), data1=R2[:, c0:c0 + n],
                                                                                       initial=Bext[:, o:o + 1], op0=ALU.mult, op1=ALU.subtract),
                         reads=["ones4", "R2", Bk], writes=[Bk])
                    S.op("dve", lambda e, Bn=Bn, c0=c0, n=n: e.tensor_tensor(out=R1[:, c0:c0 + n], in0=R1[:, c0:c0 + n], in1=Bn, op=ALU.subtract),
                         reads=["R1", Bk], writes=["R1"])
                    S.op("dve", lambda e, Gn=Gn, c0=c0, n=n, o=o: e.tensor_tensor_scan(out=Gn, data0=ones4[:, 0:1].to_broadcast([4, n]), data1=R1[:, c0:c0 + n],
                                                                                       initial=Gext[:, o:o + 1], op0=ALU.mult, op1=ALU.max),
                         reads=["ones4", "R1", Gk], writes=[Gk])
                    yield
                    gend = Gn.rearrange("p (c l) -> p c l", l=L)[:, :, L - 1:L]
                    gprev = Gext[:, o:o + n].rearrange("p (c l) -> p c l", l=L)[:, :, 0:1]
                    S.op("dve", lambda e, c0=c0, n=n, L=L, gend=gend, nch=nch: e.tensor_tensor(
                        out=R1[:, c0:c0 + n].rearrange("p (c l) -> p c l", l=L),
                        in0=R1[:, c0:c0 + n].rearrange("p (c l) -> p c l", l=L),
                        in1=gend.to_broadcast([4, nch, L]), op=ALU.subtract),
                        reads=["R1", Gk], writes=["R1"])
                    S.op("act", lambda e, c0=c0, n=n: e.activation(out=R1[:, c0:c0 + n], in_=R1[:, c0:c0 + n], func=AF.Exp, bias=LNCK, scale=1.0),
                         reads=["R1"], writes=["R1"])
                    S.op("dve", lambda e, Bn=Bn, c0=c0, n=n, L=L, gend=gend, nch=nch: e.tensor_tensor(
                        out=R3[:, c0:c0 + n].rearrange("p (c l) -> p c l", l=L),
                        in0=Bn.rearrange("p (c l) -> p c l", l=L),
                        in1=gend.to_broadcast([4, nch, L]), op=ALU.add),
                        reads=[Bk, Gk], writes=["R3"])
                    S.op("act", lambda e, c0=c0, n=n: e.activation(out=R3[:, c0:c0 + n], in_=R3[:, c0:c0 + n], func=AF.Exp, scale=-1.0),
                         reads=["R3"], writes=["R3"])
                    S.op("dve", lambda e, gend=gend, gprev=gprev, nch=nch: e.tensor_tensor(
                        out=darg[:, 0:nch].unsqueeze(2), in0=gprev, in1=gend, op=ALU.subtract),
                        reads=[Gk], writes=["darg"])
                    S.op("act", lambda e, nch=nch: e.activation(out=dec[:, 0:nch], in_=darg[:, 0:nch], func=AF.Exp),
                         reads=["darg"], writes=["dec"])
                    S.op("dve", lambda e, nch=nch, gi=gi: e.tensor_tensor(out=v3(decd[:, gi * 16:gi * 16 + nch * 4], nch),
                                                                   in0=bc(ident_f[0:4, 0:4], [4, nch, 4], 1),
                                                                   in1=bc(dec[:, 0:nch], [4, nch, 4], 2), op=ALU.mult),
                         reads=["ident_f", "dec"], writes=[f"decd{gi}"])
                    yield
                    deferred.append((g, L, nch, c0))
                    if last:
                        S.op("dve", lambda e, o=o, n=n: e.tensor_tensor(out=mfin[:], in0=Bext[:, o + n:o + n + 1], in1=Gext[:, o + n:o + n + 1],
                                                                        op=ALU.add),
                             reads=[Bk, Gk], writes=["mfin"])
                        mo_ = m_p if kind == "p" else m_s
                        ld(mo_[g["b"]].rearrange("(h o) -> h o", o=1), mfin[:], [], rk=["mfin"])
                    S.op("pool", lambda e, o=o, n=n: e.tensor_copy(out=Bext[:, o:o + 1], in_=Bext[:, o + n:o + n + 1]),
                         reads=[Bk], writes=[Bk])
                    S.op("pool", lambda e, o=o, n=n: e.tensor_copy(out=Gext[:, o:o + 1], in_=Gext[:, o + n:o + n + 1]),
                         reads=[Gk], writes=[Gk])
                    yield
                yield from g_pieces(["mq", "mk", "mv", "mo", "mz"])
                for (g, L, nch, c0) in deferred:
                    for c in range(nch):
                        sj = g["subs"][c]
                        cc = c0 + c * L
                        S.op("pe", lambda e, cc=cc, L=L, sj=sj: e.matmul(pG[0:L, sj * 16:sj * 16 + 4], lhsT=R1[:, cc:cc + L],
                                                                         rhs=ident_f[0:4, 0:4], start=True, stop=True),
                             reads=["R1", "ident_f"], writes=["pG"])
                        S.op("pe", lambda e, cc=cc, L=L, sj=sj: e.matmul(pG[0:L, sj * 16 + 4:sj * 16 + 8], lhsT=R3[:, cc:cc + L],
                                                                         rhs=ident_f[0:4, 0:4], start=True, stop=True),
                             reads=["R3", "ident_f"], writes=["pG"])
                        S.op("pe", lambda e, c=c, sj=sj, gi=g["gi"]: e.matmul(pG[:, sj * 16 + 8:sj * 16 + 12], lhsT=ones4[:, 0:128],
                                                                              rhs=decd[:, gi * 16 + c * 4:gi * 16 + (c + 1) * 4], start=True, stop=True),
                             reads=[f"decd{g['gi']}", "ones4"], writes=["pG"])
                    s0, s1 = g["subs"][0], g["subs"][-1] + 1
                    S.op("dve", lambda e, s0=s0, s1=s1, L=L: e.tensor_copy(out=gcv[0:L, s0:s1, 0:8], in_=v3(pG[0:L, 0:64], 4)[:, s0:s1, 0:8]),
                         reads=["pG"], writes=["gcols"])
                    S.op("dve", lambda e, s0=s0, s1=s1: e.tensor_copy(out=gcv[:, s0:s1, 8:12], in_=v3(pG[:, 0:64], 4)[:, s0:s1, 8:12]),
                         reads=["pG"], writes=["gcols"])
                    yield

            def g_pl():
                if first and kind == "p":
                    S.op("pool", lambda e: e.memset(KTv[:, 1], 0.0), writes=[f"KT1_{r}" for r in range(4)])
                    S.op("pool", lambda e: e.memset(Vv[:, 1], 0.0), writes=[f"V1_{j}" for j in range(4)])
                yield from g_pieces(["aq", "ak", "av", "az"])
                if last:
                    for j, su in enumerate(subs):
                        n, c0 = su["n"], su["c0"]
                        for r in range(4):
                            S.op("pe", lambda e, r=r, n=n, c0=c0: e.transpose(pT[0:n, r * 128:(r + 1) * 128],
                                                                              KTv[:, cur_slot, r, c0:c0 + n], identb[:, :]),
                                 reads=[f"KT{cur_slot}_{r}", "identb"], writes=["pT"])
                        S.op("dve", lambda e, n=n: e.tensor_copy(out=scr_a[0:n, :], in_=pT[0:n, 0:512]), reads=["pT"], writes=["scr_a"])
                        ko = k_p[su["b"], 128 * j:128 * j + n, :] if kind == "p" else k_s[su["b"], :, :]
                        ld(ko, scr_a[0:n, :], [], rk=["scr_a"])
                        yield

            def conv_tail(ch):
                pb2, pk2 = next_acc()
                for g in groups:
                    zo, c0, n = g["zoff"], g["c0"], g["n"]
                    for jt in range(4):
                        S.op("pe", lambda e, pb2=pb2, ch=ch, jt=jt, zo=zo, c0=c0, n=n: e.matmul(
                            pb2[:, c0:c0 + n], lhsT=diagv[:, ch, jt, :], rhs=zv[:, ch, zo + jt:zo + jt + n],
                            start=(jt == 0), stop=False), reads=[f"z{ch}", f"diagW{ch * 4 + jt}"], writes=[pk2])
                    S.op("pe", lambda e, pb2=pb2, ch=ch, c0=c0, n=n: e.matmul(
                        pb2[:, c0:c0 + n], lhsT=cbrow[0:1, ch * 128:(ch + 1) * 128], rhs=half_row[0:1, 0:n],
                        start=False, stop=True), reads=["cbrow", "half_row"], writes=[pk2])
                S.op("act", lambda e, pb2=pb2: e.activation(out=tht[:, 0:NT], in_=pb2[:, 0:NT], func=AF.Tanh),
                     reads=[pk2], writes=["tht"])
                S.op("dve", lambda e, pb2=pb2, ch=ch: e.scalar_tensor_tensor(out=qkv[:, ch, 0:NT], in0=tht[:, 0:NT], scalar=1.0,
                                                                             in1=pb2[:, 0:NT], op0=ALU.add, op1=ALU.mult),
                     reads=["tht", pk2], writes=[f"qk{ch}"])
                if kind == "p" and not last:
                    S.op("pool", lambda e, ch=ch: e.tensor_copy(out=zv[:, ch, 0:3], in_=zv[:, ch, 512:515]),
                         reads=[f"z{ch}"], writes=[f"z{ch}"])

            def qk_tail(name, c4):
                qb = c4 % 2
                S.op("pe", lambda e, qb=qb: e.matmul(pG[:, 0:NT], lhsT=blk64[:], rhs=qsq2[qb][:, 0:NT], start=True, stop=True),
                     reads=[f"qsq{qb}", "blk64"], writes=["pG"])
                S.op("act", lambda e: e.activation(out=rstd[:, 0:NT], in_=pG[:, 0:NT], func=AF.Ln, bias=EPS, scale=1.0 / 64),
                     reads=["pG"], writes=["rstd"])
                S.op("act", lambda e: e.activation(out=rstd[:, 0:NT], in_=rstd[:, 0:NT], func=AF.Exp, scale=-0.5),
                     reads=["rstd"], writes=["rstd"])
                if name == "aq":
                    for hh in range(2):
                        S.op("dve", lambda e, c4=c4, hh=hh, qb=qb: e.scalar_tensor_tensor(
                            out=aqv[hh * 64:hh * 64 + 64, 2 * c4 + hh, 0:NT], in0=qraw2[qb][hh * 64:hh * 64 + 64, 0:NT],
                            scalar=gqk[hh * 64:hh * 64 + 64, 0:1], in1=rstd[hh * 64:hh * 64 + 64, 0:NT],
                            op0=ALU.mult, op1=ALU.mult),
                            reads=[f"qraw{qb}", "gqk", "rstd"], writes=[f"aq{c4}"])
                else:
                    dst = KTv[:, cur_slot, c4, 0:NT]
                    S.op("dve", lambda e, dst=dst, qb=qb: e.scalar_tensor_tensor(out=dst, in0=qraw2[qb][:, 0:NT], scalar=gqk[:, 1:2],
                                                                                 in1=rstd[:, 0:NT], op0=ALU.mult, op1=ALU.mult),
                         reads=[f"qraw{qb}", "gqk", "rstd"], writes=[f"KT{cur_slot}_{c4}"])

            def g_pieces(names):
                for name in names:
                    NACC = 3 if name in ("aq", "ak", "av", "az") else 2
                    wt, wk = use_piece(name)
                    wv = v3(wt[:], 8)
                    if name in FMC:
                        for c4 in range(4):
                            ch = FMC[name] + c4
                            pb, pk = next_acc(NACC)
                            for kc in range(8):
                                S.op("pe", lambda e, pb=pb, wv=wv, kc=kc, c4=c4: e.matmul(pb[:, 0:NT], lhsT=wv[:, kc, c4 * 128:(c4 + 1) * 128],
                                                                                          rhs=hTv[:, kc, 0:NT], start=(kc == 0), stop=(kc == 7)),
                                     reads=hTk + [wk], writes=[pk])
                            if name in ("mq", "mk"):
                                for g in groups:
                                    zo, c0, n = g["zoff"], g["c0"], g["n"]
                                    S.op("act", lambda e, pb=pb, ch=ch, zo=zo, c0=c0, n=n: e.activation(
                                        out=zv[:, ch, zo + 3:zo + 3 + n], in_=pb[:, c0:c0 + n], func=AF.Identity,
                                        bias=bfm[:, ch:ch + 1], scale=1.0), reads=[pk, "bfm"], writes=[f"z{ch}"])
                                    if last:
                                        S.op("act", lambda e, pb=pb, ch=ch, c0=c0, n=n, gi=g["gi"]: e.activation(
                                            out=convov[:, gi, ch, :], in_=pb[:, c0 + n - 3:c0 + n], func=AF.Identity,
                                            bias=bfm[:, ch:ch + 1], scale=1.0), reads=[pk, "bfm"], writes=[f"convo{g['gi']}"])
                                if c4 > 0:
                                    conv_tail(ch - 1)
                                if c4 == 3:
                                    yield
                                    conv_tail(ch)
                            else:
                                qb = c4 % 2
                                S.op("act", lambda e, pb=pb, ch=ch, qb=qb: e.activation(out=qsq2[qb][:, 0:NT], in_=pb[:, 0:NT], func=AF.Square,
                                                                                        bias=bfm[:, ch:ch + 1], scale=1.0),
                                     reads=[pk, "bfm"], writes=[f"qsq{qb}", "pb_serial"])
                                S.op("dve", lambda e, pb=pb, ch=ch, qb=qb: e.tensor_tensor(out=qraw2[qb][:, 0:NT], in0=pb[:, 0:NT],
                                                                                           in1=bfm[:, ch:ch + 1].to_broadcast([128, NT]), op=ALU.add),
                                     reads=[pk, "bfm", "pb_serial"], writes=[f"qraw{qb}"])
                                if c4 > 0:
                                    qk_tail(name, c4 - 1)
                                if c4 == 3:
                                    yield
                                    qk_tail(name, 3)
                            yield
                    else:
                        tmi = TM_IDX[name]
                        for j, su in enumerate(subs):
                            n, c0 = su["n"], su["c0"]
                            pb, pk = next_acc(NACC)
                            for kc in range(8):
                                S.op("pe", lambda e, pb=pb, wv=wv, kc=kc, n=n, c0=c0: e.matmul(pb[0:n, :], lhsT=hTv[:, kc, c0:c0 + n],
                                                                                               rhs=wv[:, kc, :], start=(kc == 0), stop=False),
                                     reads=[hTk[j], wk], writes=[pk])
                            S.op("pe", lambda e, pb=pb, n=n, tmi=tmi: e.matmul(pb[0:n, :], lhsT=sel5[0:5, tmi * 128:tmi * 128 + n],
                                                                               rhs=btm[0:5, :], start=False, stop=True),
                                 reads=["sel5", "btm"], writes=[pk])
                            if name == "mv":
                                S.op("act", lambda e, pb=pb, n=n, j=j: e.activation(out=vaugv[0:n, j, :, 0:128], in_=v3(pb[0:n, :], 4),
                                                                                    func=AF.Copy), reads=[pk], writes=[f"vaug{j}"])
                            elif name == "mo":
                                S.op("act", lambda e, pb=pb, n=n, j=j: e.activation(out=thov[0:n, j, :], in_=pb[0:n, :], func=AF.Tanh, scale=0.5),
                                     reads=[pk], writes=[f"tho{j}"])
                            elif name == "mz":
                                S.op("act", lambda e, pb=pb, n=n: e.activation(out=thz[0:n, :], in_=pb[0:n, :], func=AF.Tanh, scale=0.5),
                                     reads=[pk], writes=["thz"])
                                S.op("dve", lambda e, pb=pb, n=n: e.scalar_tensor_tensor(out=scr_mz[0:n, :], in0=thz[0:n, :], scalar=1.0,
                                                                                         in1=pb[0:n, :], op0=ALU.add, op1=ALU.mult),
                                     reads=["thz", pk], writes=["scr_mz"])
                                S.op("dve", lambda e, n=n, j=j: e.scalar_tensor_tensor(out=GMv[0:n, j, :], in0=thov[0:n, j, :], scalar=1.0,
                                                                                       in1=scr_mz[0:n, :], op0=ALU.add, op1=ALU.mult),
                                     reads=[f"tho{j}", "scr_mz"], writes=[f"GM{j}"])
                            elif name == "av":
                                vdst = Vv[0:n, cur_slot, j, :, 0:64]
                                odst = Vv[0:n, cur_slot, j, :, 64:65]
                                vk = f"V{cur_slot}_{j}"
                                S.op("act", lambda e, pb=pb, n=n, vdst=vdst: e.activation(out=vdst, in_=v3(pb[0:n, :], 8), func=AF.Copy),
                                     reads=[pk], writes=[vk])
                                S.op("pool", lambda e, odst=odst: e.memset(odst, 1.0), writes=[vk])
                                if last:
                                    S.op("act", lambda e, pb=pb, n=n: e.activation(out=scr_a[0:n, :], in_=pb[0:n, :], func=AF.Copy),
                                         reads=[pk], writes=["scr_a"])
                                    vo = v_p[su["b"], 128 * j:128 * j + n, :] if kind == "p" else v_s[su["b"], :, :]
                                    ld(vo, scr_a[0:n, :], [], rk=["scr_a"])
                            elif name == "az":
                                S.op("act", lambda e, pb=pb, n=n: e.activation(out=thz[0:n, :], in_=pb[0:n, :], func=AF.Tanh, scale=0.5),
                                     reads=[pk], writes=["thz"])
                                S.op("dve", lambda e, pb=pb, n=n, j=j: e.scalar_tensor_tensor(out=GAv[0:n, j, :], in0=thz[0:n, :], scalar=1.0,
                                                                                              in1=pb[0:n, :], op0=ALU.add, op1=ALU.mult),
                                     reads=["thz", pk], writes=[f"GA{j}"])
                            yield
                    if name == "mk" and last:
                        co = conv_p if kind == "p" else conv_s
                        for g in groups:
                            for hf in range(2):
                                for c4 in range(4):
                                    S.op("pe", lambda e, hf=hf, c4=c4, gi=g["gi"]: e.matmul(pG[0:3, c4 * 128:(c4 + 1) * 128],
                                                                                            lhsT=convov[:, gi, hf * 4 + c4, :], rhs=ident_f[:, :],
                                                                                            start=True, stop=True),
                                         reads=[f"convo{g['gi']}", "ident_f"], writes=["pG"])
                                S.op("dve", lambda e, hf=hf: e.tensor_copy(out=stage[0:3, hf * 512:(hf + 1) * 512], in_=pG[0:3, 0:512]),
                                     reads=["pG"], writes=["scr_g1", "scr_g2"])
                            ld(co[g["b"]], stage[0:3, :], [], rk=["scr_g1", "scr_g2"])
                            yield

            def g_m():
                for j, su in enumerate(subs):
                    L, c0 = su["n"], su["c0"]
                    if kind == "s":
                        for h in range(4):
                            ld(Cfv[:, h, 0:128], st_C[su["b"], h], ["C_f"])
                        ld(stage[0:4, 0:128], st_n[su["b"]], ["scr_g1", "scr_g2"], rk=["EBS3", "EBSn"])
                        S.op("pe", lambda e: e.matmul(pS2[:, 0:4], lhsT=stage[0:4, 0:128], rhs=ident_f[0:4, 0:4], start=True, stop=True),
                             reads=["scr_g1", "scr_g2", "ident_f"], writes=["pS2"])
                        S.op("dve", lambda e: e.tensor_copy(out=Cfv[:, :, 128:129], in_=pS2[:, 0:4].unsqueeze(2)), reads=["pS2"], writes=["C_f"])
                    ucol = gcv[0:L, j, 0:4]
                    fcol = gcv[0:L, j, 4:8]
                    dcl = gcv[:, j, 8:12]
                    S.op("pool", lambda e, dcl=dcl: e.tensor_tensor(out=Cfv[:, :, :], in0=Cfv[:, :, :], in1=bc(dcl, [128, 4, 129], 2), op=ALU.mult),
                         reads=["C_f", "gcols"], writes=["C_f"])
                    S.op("act", lambda e: e.activation(out=Cdec_b[:], in_=C_f[:], func=AF.Copy), reads=["C_f"], writes=["Cdec_b"])
                    S.op("pool", lambda e, L=L, j=j, ucol=ucol: e.tensor_tensor(out=vuv[0:L, :, :], in0=vaugv[0:L, j, :, :],
                                                                                in1=bc(ucol, [L, 4, 129], 2), op=ALU.mult),
                         reads=[f"vaug{j}", "gcols"], writes=["vu"])
                    for h in range(4):
                        S.op("pe", lambda e, h=h, L=L, c0=c0: e.matmul(pS[0:L, h * 128:h * 128 + L], lhsT=qkv[:, 4 + h, c0:c0 + L],
                                                                       rhs=qkv[:, h, c0:c0 + L], start=True, stop=True),
                             reads=[f"qk{h}", f"qk{4 + h}"], writes=["pS"])
                    for h in range(4):
                        S.op("pe", lambda e, h=h, L=L, c0=c0: e.transpose(pT[0:L, 512 + h * 128:512 + (h + 1) * 128], qkv[:, 4 + h, c0:c0 + L],
                                                                          identb[:, :]),
                             reads=[f"qk{4 + h}", "identb"], writes=["pT"])
                    S.op("act", lambda e, L=L: e.activation(out=ktok[0:L, :], in_=pT[0:L, 512:1024], func=AF.Copy), reads=["pT"], writes=["ktok"])
                    yield
                    S.op("dve", lambda e, L=L: e.tensor_tensor(out=v3(SW[0:L, :], 4)[:, :, 0:L], in0=v3(pS[0:L, :], 4)[:, :, 0:L],
                                                               in1=bc(cmask[0:L, 0:L], [L, 4, L], 1), op=ALU.mult),
                         reads=["pS", "cmask"], writes=["SW"])
                    yield
                    for h in range(4):
                        px = pX[h // 2]
                        o0 = (h % 2) * 129
                        S.op("pe", lambda e, h=h, L=L, px=px, o0=o0: e.matmul(px[0:L, o0:o0 + 129], lhsT=v3(SW[0:L, :], 4)[:, h, 0:L],
                                                                              rhs=vuv[0:L, h, :], start=True, stop=False),
                             reads=["SW", "vu"], writes=[f"pX{h // 2}"])
                        S.op("pe", lambda e, h=h, L=L, px=px, o0=o0, c0=c0: e.matmul(px[0:L, o0:o0 + 129], lhsT=qkv[:, h, c0:c0 + L],
                                                                                     rhs=Cdbv[:, h, :], start=False, stop=True),
                             reads=[f"qk{h}", "Cdec_b"], writes=[f"pX{h // 2}"])
                    yield
                    for hb in range(2):
                        S.op("act", lambda e, hb=hb, L=L: e.activation(
                            out=dcol[0:L, hb * 2:hb * 2 + 2].unsqueeze(2), in_=v3(pX[hb][0:L, 0:258], 2)[:, :, 128:129], func=AF.Abs),
                            reads=[f"pX{hb}"], writes=["dcol"])
                    for h in range(4):
                        px = pX[h // 2]
                        o0 = (h % 2) * 129
                        S.op("act", lambda e, h=h, L=L, px=px, o0=o0: e.activation(out=hm[0:L, h * 128:(h + 1) * 128], in_=px[0:L, o0:o0 + 128],
                                                                                   func=AF.Square, accum_out=sscol[0:L, h:h + 1]),
                             reads=[f"pX{h // 2}"], writes=["hm", "sscol"])
                    S.op("dve", lambda e, L=L, fcol=fcol: e.tensor_tensor(out=dcol[0:L, :], in0=dcol[0:L, :], in1=fcol, op=ALU.max),
                         reads=["dcol", "gcols"], writes=["dcol"])
                    S.op("dve", lambda e, L=L: e.reciprocal(out=rcol[0:L, :], in_=dcol[0:L, :]), reads=["dcol"], writes=["rcol"])
                    S.op("dve", lambda e, L=L: e.tensor_tensor(out=tcol[0:L, :], in0=rcol[0:L, :], in1=rcol[0:L, :], op=ALU.mult),
                         reads=["rcol"], writes=["tcol"])
                    yield
                    S.op("dve", lambda e, L=L: e.tensor_tensor(out=tcol[0:L, :], in0=tcol[0:L, :], in1=sscol[0:L, :], op=ALU.mult),
                         reads=["tcol", "sscol"], writes=["tcol"])
                    S.op("act", lambda e, L=L: e.activation(out=t2col[0:L, :], in_=tcol[0:L, :], func=AF.Ln, bias=4.0 * EPS, scale=4.0 / 128),
                         reads=["tcol"], writes=["t2col"])
                    S.op("act", lambda e, L=L: e.activation(out=t2col[0:L, :], in_=t2col[0:L, :], func=AF.Exp, scale=-0.5),
                         reads=["t2col"], writes=["t2col"])
                    S.op("dve", lambda e, L=L: e.tensor_tensor(out=rho[0:L, :], in0=rcol[0:L, :], in1=t2col[0:L, :], op=ALU.mult),
                         reads=["rcol", "t2col"], writes=["rho"])
                    yield
                    for h in range(4):
                        px = pX[h // 2]
                        o0 = (h % 2) * 129
                        S.op("dve", lambda e, h=h, L=L, px=px, o0=o0, j=j: e.scalar_tensor_tensor(
                            out=hm[0:L, h * 128:(h + 1) * 128], in0=px[0:L, o0:o0 + 128], scalar=rho[0:L, h:h + 1],
                            in1=GMv[0:L, j, h * 128:(h + 1) * 128], op0=ALU.mult, op1=ALU.mult),
                            reads=[f"pX{h // 2}", "rho", f"GM{j}"], writes=["hm"])
                    yield
                    yield
                    for h in range(4):
                        px = pX[h // 2]
                        o0 = (h % 2) * 129
                        S.op("pe", lambda e, h=h, L=L, px=px, o0=o0: e.matmul(px[:, o0:o0 + 129], lhsT=ktok[0:L, h * 128:(h + 1) * 128],
                                                                              rhs=vuv[0:L, h, :], start=True, stop=True),
                             reads=["ktok", "vu"], writes=[f"pX{h // 2}"])
                    yield
                    for h in range(4):
                        S.op("pe", lambda e, h=h, L=L: e.transpose(pT[:, h * 128:h * 128 + L], hm[0:L, h * 128:(h + 1) * 128], identb[0:L, 0:L]),
                             reads=["hm", "identb"], writes=["pT"])
                    for hb in range(2):
                        S.op("dve", lambda e, hb=hb: e.tensor_tensor(out=Cfv[:, hb * 2:hb * 2 + 2, :], in0=Cfv[:, hb * 2:hb * 2 + 2, :],
                                                                     in1=v3(pX[hb][:, 0:258], 2), op=ALU.add),
                             reads=["C_f", f"pX{hb}"], writes=["C_f"])
                    S.op("dve", lambda e, L=L, c0=c0: e.tensor_tensor(out=hmTv[:, :, c0:c0 + L], in0=v3(pT[:, 0:512], 4)[:, :, 0:L],
                                                                      in1=bc(mg[:, 0:4], [128, 4, L], 2), op=ALU.mult),
                         reads=["pT", "mg"], writes=[hmk[j]])
                    if last and (kind == "s" or j == nsub - 1):
                        Co, no = (C_p, n_p) if kind == "p" else (C_s, n_s)
                        for h in range(4):
                            ld(Co[su["b"], h], Cfv[:, h, 0:128], [], rk=["C_f"])
                        S.op("pe", lambda e: e.matmul(pS2[0:4, 0:128], lhsT=Cfv[:, :, 128], rhs=ident_f[:, :], start=True, stop=True),
                             reads=["C_f", "ident_f"], writes=["pS2"])
                        S.op("dve", lambda e: e.tensor_copy(out=stage[0:4, 0:128], in_=pS2[0:4, 0:128]), reads=["pS2"], writes=["scr_g1", "scr_g2"])
                        ld(no[su["b"]], stage[0:4, 0:128], [], rk=["scr_g1", "scr_g2"])
                    yield

            def g_a():
                for j, su in enumerate(subs):
                    n, c0 = su["n"], su["c0"]
                    if kind == "p":
                        order = (0, 3, 4, 1, 2)
                        keyt = []
                        for jk in order:
                            gk = 4 * ti + j + jk - 4
                            keyt.append(((gk // 4) % 2, gk % 4))
                        PT6 = PT[:].rearrange("p (u pl j hh q) -> p u pl j hh q", u=2, pl=2, j=5, hh=2)

                        def pv_unit(hg, keyt=keyt):
                            for h in range(4 * hg, 4 * hg + 4):
                                px = pX[h // 4]
                                o0 = (h % 4) * 65
                                pl, hh = (h % 4) // 2, h % 2
                                for jj in range(5):
                                    slot, kt = keyt[jj]
                                    S.op("pe", lambda e, h=h, jj=jj, px=px, o0=o0, slot=slot, kt=kt, pl=pl, hh=hh, hg=hg: e.matmul(
                                        px[:, o0:o0 + 65], lhsT=PT6[:, hg, pl, jj, hh, :], rhs=Vv[:, slot, kt, h, :], start=(jj == 0), stop=(jj == 4)),
                                        reads=[f"PT{nb}" for nb in range(5 * hg, 5 * hg + 5)] + [f"V{slot}_{kt}"], writes=[f"pX{h // 4}"])

                        for hg in range(2):
                            for it in range(10):
                                pl, jj = it // 5, it % 5
                                pr = 2 * hg + pl
                                slot, kt = keyt[jj]
                                idx2 = 20 * hg + 2 * it
                                bank = sc_i[0] % 3
                                bk = ("pS", "pS2", "pG")[bank]
                                S.op("pe", lambda e, pr=pr, slot=slot, kt=kt, bank=bank, idx2=idx2, c0=c0: e.matmul(
                                    pSS3[bank][:, (idx2 % 4) * 128:(idx2 % 4) * 128 + 256], lhsT=KTv[:, slot, pr, kt * 128:(kt + 1) * 128],
                                    rhs=aqv[:, 2 * pr:2 * pr + 2, c0:c0 + 128], start=True, stop=True),
                                    reads=[f"KT{slot}_{pr}", f"aq{pr}"], writes=[bk])
                                if it % 2 == 1:
                                    nb = idx2 // 4
                                    S.op("act", lambda e, nb=nb, bank=bank: e.activation(out=PT[:, nb * 512:(nb + 1) * 512], in_=pSS3[bank][:, :],
                                                                                         func=AF.Exp, scale=0.125),
                                         reads=[bk], writes=[f"PT{nb}"])
                                    sc_i[0] += 1
                                    yield
                            pk_ = [f"PT{nb}" for nb in range(5 * hg, 5 * hg + 5)]
                            S.op("pool", lambda e, hg=hg: e.memset(PT6[0:64, hg, :, 0, :, 64:128], 0.0), reads=pk_, writes=pk_)
                            for pl in range(2):
                                pr = 2 * hg + pl
                                S.op("dve", lambda e, hg=hg, pl=pl, pr=pr: e.tensor_tensor(
                                    out=PT6[:, hg, pl, 1:3, :, :], in0=PT6[:, hg, pl, 1:3, :, :],
                                    in1=EBMv[:, 2 * pr:2 * pr + 2, :, :].rearrange("p hh j q -> p j hh q"), op=ALU.mult),
                                    reads=pk_ + ["EBM"], writes=pk_)
                            if hg == 1:
                                pv_unit(0)
                            yield
                        yield
                        pv_unit(1)
                        yield
                    else:
                        bb = su["b"]
                        for tk in range(4):
                            ld(stage[:, 0:512], ck[bb, tk * 128:(tk + 1) * 128, :], ["scr_g1", "scr_g2"], rk=["EBS3", "EBSn"])
                            S.op("dve", lambda e: e.tensor_copy(out=PT[:, 0:512], in_=stage[:, 0:512]), reads=["scr_g1", "scr_g2"], writes=PTK)
                            for r in range(4):
                                S.op("pe", lambda e, r=r: e.transpose(pT[:, r * 128:(r + 1) * 128], PT[:, r * 128:(r + 1) * 128], identb[:, :]),
                                     reads=PTK + ["identb"], writes=["pT"])
                            S.op("dve", lambda e, tk=tk: e.tensor_copy(out=KTv[:, 0, :, tk * 128:(tk + 1) * 128], in_=v3(pT[:, 0:512], 4)),
                                 reads=["pT"], writes=[f"KT0_{r}" for r in range(4)])
                            ld(stage[:, 512:1024], cv[bb, tk * 128:(tk + 1) * 128, :], ["scr_g2"], rk=["EBS3", "EBSn"])
                            S.op("act", lambda e, tk=tk: e.activation(out=Vv[:, 0, tk, :, 0:64], in_=v3(stage[:, 512:1024], 8), func=AF.Copy), reads=["scr_g2"],
                                 writes=[f"V0_{tk}"])
                            S.op("pool", lambda e, tk=tk: e.memset(Vv[:, 0, tk, :, 64:65], 1.0), writes=[f"V0_{tk}"])
                            yield
                        for h in range(8):
                            bank = h // 4
                            for tk in range(5):
                                o0 = ((h % 4) * 5 + tk) * 16
                                if tk < 4:
                                    S.op("pe", lambda e, h=h, tk=tk, bank=bank, o0=o0, c0=c0: e.matmul(
                                        pSS[bank][:, o0:o0 + 16], lhsT=KTv[:, 0, h // 2, tk * 128:(tk + 1) * 128],
                                        rhs=aqv[:, h, c0:c0 + 16], start=True, stop=True),
                                        reads=[f"KT0_{h // 2}", f"aq{h // 2}"], writes=["pS" if bank == 0 else "pS2"])
                                else:
                                    S.op("pe", lambda e, h=h, bank=bank, o0=o0, c0=c0: e.matmul(
                                        pSS[bank][0:16, o0:o0 + 16], lhsT=KTv[:, 1, h // 2, c0:c0 + 16],
                                        rhs=aqv[:, h, c0:c0 + 16], start=True, stop=True),
                                        reads=[f"KT1_{h // 2}", f"aq{h // 2}"], writes=["pS" if bank == 0 else "pS2"])
                        PTs = PT[:, 0:640].rearrange("p (h t q) -> p h t q", h=8, t=5)
                        for bank in range(2):
                            S.op("act", lambda e, bank=bank: e.activation(out=v3(PT[:, bank * 320:(bank + 1) * 320], 4)[:, :, 0:64],
                                                                          in_=v3(pSS[bank][:, 0:320], 4)[:, :, 0:64], func=AF.Exp, scale=0.125),
                                 reads=["pS" if bank == 0 else "pS2"], writes=PTK)
                            S.op("act", lambda e, bank=bank: e.activation(out=v3(PT[0:16, bank * 320:(bank + 1) * 320], 4)[:, :, 64:80],
                                                                          in_=v3(pSS[bank][0:16, 0:320], 4)[:, :, 64:80], func=AF.Exp, scale=0.125),
                                 reads=["pS" if bank == 0 else "pS2"], writes=PTK)
                        S.op("pool", lambda e, PTs=PTs: e.tensor_tensor(out=PTs[:, :, 3, :], in0=PTs[:, :, 3, :], in1=v3(EBS3[:], 8), op=ALU.mult),
                             reads=PTK + ["EBS3"], writes=PTK)
                        S.op("pool", lambda e, PTs=PTs: e.tensor_tensor(out=PTs[0:16, :, 4, :], in0=PTs[0:16, :, 4, :], in1=v3(EBSn[:], 8), op=ALU.mult),
                             reads=PTK + ["EBSn"], writes=PTK)
                        yield
                        for h in range(8):
                            px = pX[h // 4]
                            o0 = (h % 4) * 65
                            for tk in range(5):
                                if tk < 4:
                                    S.op("pe", lambda e, h=h, tk=tk, px=px, o0=o0, PTs=PTs: e.matmul(px[0:16, o0:o0 + 65], lhsT=PTs[:, h, tk, :],
                                                                                                     rhs=Vv[:, 0, tk, h, :], start=(tk == 0), stop=False),
                                         reads=PTK + [f"V0_{tk}"], writes=[f"pX{h // 4}"])
                                else:
                                    S.op("pe", lambda e, h=h, px=px, o0=o0, j=j, PTs=PTs: e.matmul(px[0:16, o0:o0 + 65], lhsT=PTs[0:16, h, 4, :],
                                                                                                   rhs=Vv[0:16, 1, j, h, :], start=False, stop=True),
                                         reads=PTK + [f"V1_{j}"], writes=[f"pX{h // 4}"])
                        yield
                    for hb in range(2):
                        S.op("dve", lambda e, hb=hb, n=n: e.reciprocal(out=rden[0:n, hb * 4:hb * 4 + 4].unsqueeze(2),
                                                                       in_=v3(pX[hb][0:n, 0:260], 4)[:, :, 64:65]),
                             reads=[f"pX{hb}"], writes=["rden"])
                    for hb in range(2):
                        S.op("dve", lambda e, hb=hb, n=n: e.tensor_tensor(out=v3(scr_a[0:n, hb * 256:(hb + 1) * 256], 4),
                                                                          in0=v3(pX[hb][0:n, 0:260], 4)[:, :, 0:64],
                                                                          in1=bc(rden[0:n, hb * 4:hb * 4 + 4], [n, 4, 64], 2), op=ALU.mult),
                             reads=[f"pX{hb}", "rden"], writes=["scr_a"])
                    S.op("dve", lambda e, n=n, j=j: e.tensor_tensor(out=ha[0:n, :], in0=scr_a[0:n, :], in1=GAv[0:n, j, :], op=ALU.mult),
                         reads=["scr_a", f"GA{j}"], writes=["ha"])
                    yield
                    yield
                    for r in range(4):
                        S.op("pe", lambda e, r=r, n=n: e.transpose(pT[:, r * 128:r * 128 + n], ha[0:n, r * 128:(r + 1) * 128], identb[0:n, 0:n]),
                             reads=["ha", "identb"], writes=["pT"])
                    S.op("dve", lambda e, n=n, c0=c0: e.tensor_copy(out=haTv[:, :, c0:c0 + n], in_=v3(pT[:, 0:512], 4)[:, :, 0:n]),
                         reads=["pT"], writes=[hak[j]])
                    yield

            def g_g(part="all"):
                NACC = 2 if part == "head" else 3
                for hf in range(2):
                    for which, (nm, boff, toff) in enumerate([(f"gm{hf}", 16, 0), (f"ga{hf}", 24, 4)]):
                        if part == "tail" and hf == 0:
                            continue
                        wt, wk = use_piece(nm)
                        wv_ = v3(wt[:], 8)
                        for d4 in range(4):
                            dc = 4 * hf + d4
                            pb, pk = next_acc(NACC)
                            for kc in range(8):
                                S.op("pe", lambda e, pb=pb, wv_=wv_, kc=kc, d4=d4: e.matmul(pb[:, 0:NT], lhsT=wv_[:, kc, d4 * 128:(d4 + 1) * 128],
                                                                                            rhs=hTv[:, kc, 0:NT], start=(kc == 0), stop=(kc == 7)),
                                     reads=hTk + [wk], writes=[pk])
                            S.op("act", lambda e, pb=pb, bch=boff + dc, ts_=toff + d4: e.activation(out=thgv[:, ts_, 0:NT], in_=pb[:, 0:NT], func=AF.Tanh,
                                                                                                  bias=bfm_h[:, bch:bch + 1], scale=0.5),
                                 reads=[pk, "bfm_h"], writes=[f"thg{toff + d4}"])
                            yield
                    if part == "head":
                        return
                    wbt, wbk = use_piece(f"wb{hf}")
                    wbp = wbt[:].rearrange("p (m c d) -> p m c d", m=2, c=4)
                    for d4 in range(4):
                        dc = 4 * hf + d4
                        for cc in range(4):
                            S.op("pe", lambda e, d4=d4, cc=cc, wbp=wbp: e.matmul(pG[:, 0:NT], lhsT=wbp[:, 0, cc, d4 * 128:(d4 + 1) * 128],
                                                                                 rhs=hmTv[:, cc, 0:NT], start=(cc == 0), stop=(cc == 3)),
                                 reads=hmk + [wbk], writes=["pG"])
                        for cc in range(4):
                            S.op("pe", lambda e, d4=d4, cc=cc, wbp=wbp: e.matmul(pS2[:, 0:NT], lhsT=wbp[:, 1, cc, d4 * 128:(d4 + 1) * 128],
                                                                                 rhs=haTv[:, cc, 0:NT], start=(cc == 0), stop=(cc == 3)),
                                 reads=hak + [wbk], writes=["pS2"])
                        S.op("dve", lambda e, d4=d4: e.scalar_tensor_tensor(out=scr_g1[:, 0:NT], in0=thgv[:, d4, 0:NT], scalar=1.0, in1=pG[:, 0:NT],
                                                                            op0=ALU.add, op1=ALU.mult),
                             reads=[f"thg{d4}", "pG"], writes=["scr_g1"])
                        S.op("dve", lambda e, d4=d4: e.scalar_tensor_tensor(out=scr_g2[:, 0:NT], in0=thgv[:, 4 + d4, 0:NT], scalar=1.0,
                                                                            in1=pS2[:, 0:NT], op0=ALU.add, op1=ALU.mult),
                             reads=[f"thg{4 + d4}", "pS2"], writes=["scr_g2"])
                        S.op("pool", lambda e, dc=dc: e.tensor_tensor(out=mixv[:, dc, 0:NT], in0=scr_g1[:, 0:NT], in1=scr_g2[:, 0:NT], op=ALU.add),
                             reads=["scr_g1", "scr_g2"], writes=[f"mix{dc}"])
                        yield
                yield
                yield
                units = [(eh, j) for eh in range(2) for j in range(nsub)]
                slots = {}

                def reload(u):
                    eh, j = units[u]
                    su = subs[j]
                    n = su["n"]
                    yb = yb_i[0] % 3
                    yb_i[0] += 1
                    slots[u] = yb
                    ld(ybv[0:n, yb, :], xsrc[su["b"], su["t0"]:su["t0"] + n, eh * 512:(eh + 1) * 512], [f"ybuf{yb}"])

                reload(0)
                wt = wk = wov = None
                for u, (eh, j) in enumerate(units):
                    if j == 0:
                        wt, wk = use_piece(f"wout{eh}")
                        wov = v3(wt[:], 8)
                    su = subs[j]
                    n, c0 = su["n"], su["c0"]
                    if u + 1 < len(units):
                        reload(u + 1)
                    yb = slots[u]
                    yk = f"ybuf{yb}"
                    pb, pk = next_acc(3)
                    for dc in range(8):
                        S.op("pe", lambda e, pb=pb, dc=dc, n=n, c0=c0, wov=wov: e.matmul(pb[0:n, :], lhsT=mixv[:, dc, c0:c0 + n],
                                                                                         rhs=wov[:, dc, :], start=(dc == 0), stop=(dc == 7)),
                             reads=[f"mix{dc}", wk], writes=[pk])
                    S.op("dve", lambda e, pb=pb, n=n, yb=yb: e.scalar_tensor_tensor(
                        out=ybv[0:n, yb, :], in0=pb[0:n, :], scalar=0.25, in1=ybv[0:n, yb, :], op0=ALU.mult, op1=ALU.add),
                        reads=[pk, yk], writes=[yk])
                    yo = y_p[su["b"], su["t0"]:su["t0"] + n, eh * 512:(eh + 1) * 512] if kind == "p" else y_s[su["b"], :, eh * 512:(eh + 1) * 512]
                    ld(yo, ybv[0:n, yb, :], [], rk=[yk])
                    yield

            T.xp, T.pl, T.m, T.a, T.g = g_xp, g_pl, g_m, g_a, g_g
            return T

        EARLY = ["mq", "mk", "mv", "mo", "mz"]
        LATE = ["aq", "ak", "av", "az"]
        GP = ["gm0", "ga0", "wb0", "gm1", "ga1", "wb1", "wout0", "wout1"]
        tiles = [make_tile("p", 0, ti, ti) for ti in range(4)]
        tiles.append(make_tile("s", 0, 0, 4))
        tiles += [make_tile("p", 1, ti, 5 + ti) for ti in range(4)]
        NTL = len(tiles)
        piece_seq.extend(EARLY)
        for i in range(NTL):
            piece_seq.extend(LATE)
            if i > 0:
                piece_seq.extend(GP)
            if i + 1 < NTL:
                piece_seq.extend(EARLY)
        piece_seq.extend(GP)

        def chain(*gens):
            for g in gens:
                yield from g

        def count_steps(genfunc_list):
            snap = (acc_i[0], xi_i[0], yb_i[0], acc3_i[0], sc_i[0])
            S.dry = True
            n = 0
            for gf in genfunc_list:
                for _ in gf():
                    n += 1
            S.dry = False
            acc_i[0], xi_i[0], yb_i[0], acc3_i[0], sc_i[0] = snap
            return max(n, 1)

        def interleave(items, bias=None):
            chains = []
            for ci, gfl in enumerate(items):
                tot = count_steps(gfl)
                chains.append(dict(gen=chain(*[gf() for gf in gfl]), tot=tot, done=0, alive=True,
                                   bias=(bias[ci] if bias else 1.0)))
            while any(c["alive"] for c in chains):
                c = min((c for c in chains if c["alive"]), key=lambda c: c["bias"] * (c["done"] + 1) / c["tot"])
                try:
                    next(c["gen"])
                    c["done"] += 1
                except StopIteration:
                    c["alive"] = False

        interleave([[tiles[0].xp]])
        build_ebm()
        for i, T in enumerate(tiles):
            bulk = [T.pl, tiles[i - 1].g] if i > 0 else [T.pl]
            interleave([[T.m], bulk])
            if i + 1 < NTL:
                interleave([[T.a], [tiles[i + 1].xp]], bias=[0.8, 1.0])
            else:
                interleave([[T.a], [lambda T=T: T.g("head")]])
        interleave([[lambda: tiles[-1].g("tail")]])
        assert piece_pos[0] == len(piece_seq), (piece_pos[0], len(piece_seq))
        S.emit()
    return nc


_CACHE = {}


def _host_consts(rel_bias):
    qi = np.arange(128)[None, :]
    ki = np.arange(128)[:, None]
    idx4 = np.clip(qi - ki, -128, 128) + 128
    idx3 = np.clip(qi - ki + 128, -128, 128) + 128
    relb_t = np.stack([rel_bias[:, idx3], rel_bias[:, idx4]], axis=1)
    relb_t = np.ascontiguousarray(relb_t.transpose(2, 0, 1, 3)).reshape(128, 8 * 2 * 128)
    relb_c = np.ascontiguousarray(np.broadcast_to(rel_bias[None, :, 256], (128, 8)))
    q16 = np.arange(16)[None, :]
    idx_s3 = np.clip(q16 + 128 - ki, -128, 128) + 128
    relb_s3 = np.ascontiguousarray(rel_bias[:, idx_s3].transpose(1, 0, 2)).reshape(128, 8 * 16)
    k16 = np.arange(16)[:, None]
    idx_sn = np.clip(q16 - k16, -128, 128) + 128
    relb_sn = np.ascontiguousarray(rel_bias[:, idx_sn].transpose(1, 0, 2)).reshape(16, 8 * 16)
    return relb_t, relb_c, relb_s3, relb_sn


def kernel(x_prompt, x_sample, state_mlstm_C, state_mlstm_n, state_mlstm_m, state_mlstm_conv,
           cache_attn_k, cache_attn_v, norm_g, w_in, b_in, conv_w, conv_b, m_head_g,
           q_norm_g, k_norm_g, rel_bias, w_bm, w_ba, w_out):
    f = lambda a: np.ascontiguousarray(np.asarray(a, dtype=np.float32))
    x_prompt, x_sample = f(x_prompt), f(x_sample)
    w_in0, b_in0 = f(w_in)[0], f(b_in)[0]
    wbm0, wba0, wout0 = f(w_bm)[0], f(w_ba)[0], f(w_out)[0]

    def kmajor(w, nk):
        return w.reshape(nk, 128, w.shape[1]).transpose(1, 0, 2)

    pieces = []
    for name in PIECES:
        if name in W_COL:
            c = W_COL[name]
            pieces.append(kmajor(w_in0[:, c:c + 512], 8).reshape(128, 4096))
        elif name.startswith("wb"):
            hf = int(name[2])
            a = kmajor(wbm0[:, hf * 512:(hf + 1) * 512], 4).reshape(128, 2048)
            bb = kmajor(wba0[:, hf * 512:(hf + 1) * 512], 4).reshape(128, 2048)
            pieces.append(np.concatenate([a, bb], axis=1))
        else:
            eh = int(name[4])
            pieces.append(kmajor(wout0[:, eh * 512:(eh + 1) * 512], 8).reshape(128, 4096))
    w_all = np.stack(pieces)
    w_g = np.ascontiguousarray(kmajor(w_in0[:, 2560:2568], 8)).reshape(128, 64)
    fm_cols = [0, 512, 2568, 3080, 4616, 5128, 5640, 6152]
    b_fm = np.concatenate([b_in0[c:c + 512].reshape(4, 128).T for c in fm_cols], axis=1)
    b_g = np.stack([b_in0[2560:2564], b_in0[2564:2568]], axis=1)
    b_tm = np.stack([b_in0[c:c + 512] for c in (1024, 1536, 2048, 3592, 4104)])
    sel5 = np.repeat(np.eye(5, dtype=np.float32), 128, axis=1)
    gx = f(norm_g)[0].reshape(8, 128).T
    cw = f(conv_w)[0].reshape(4, 8, 128).transpose(2, 1, 0).reshape(128, 32)
    cb = f(conv_b)[0][None, :]
    mg = f(m_head_g)[0].T
    gqk = np.stack([np.tile(f(q_norm_g)[0], 2), np.tile(f(k_norm_g)[0], 2)], axis=1)
    relb_t, relb_c, relb_s3, relb_sn = _host_consts(f(rel_bias)[0])
    ident = np.eye(128, dtype=np.float32)
    ki = np.arange(128)[:, None]; qi = np.arange(128)[None, :]
    cmask = (ki <= qi).astype(np.float32)
    blk = ((ki // 64) == (qi // 64)).astype(np.float32)
    vm0 = 1.0 - ((ki < 64) & (qi >= 64)).astype(np.float32)
    vm3 = np.ones((128, 128), np.float32)
    vm4 = 1.0 - ((ki >= 64) & (qi < 64)).astype(np.float32)
    vmask = np.stack([vm0, vm3, vm4], axis=1).reshape(128, 384)
    shared = dict(w_all=w_all, w_g=w_g, b_fm=b_fm, b_g=b_g, b_tm=b_tm, sel5=sel5, gx=gx, cw=cw, cb=cb, mg=mg, gqk=gqk,
                  relb_t=relb_t, relb_c=relb_c, relb_s3=relb_s3, relb_sn=relb_sn,
                  ident=ident, cmask=cmask, blk64=blk, vmask=vmask)
    shared = {k: np.ascontiguousarray(v, dtype=np.float32) for k, v in shared.items()}
    sC, sn, sm, sconv = f(state_mlstm_C)[0], f(state_mlstm_n)[0], f(state_mlstm_m)[0], f(state_mlstm_conv)[0]
    ckf, cvf = f(cache_attn_k)[0].reshape(16, 512, 512), f(cache_attn_v)[0].reshape(16, 512, 512)
    in_maps = []
    for c in range(NCORES):
        sl = slice(2 * c, 2 * c + 2)
        m = dict(shared)
        m.update(x_p=x_prompt[sl], x_s=x_sample[sl], st_C=sC[sl], st_n=sn[sl], st_m=sm[sl], st_conv=sconv[sl],
                 ck=ckf[sl], cv=cvf[sl])
        in_maps.append({k: np.ascontiguousarray(v) for k, v in m.items()})
    if "nc" not in _CACHE:
        _CACHE["nc"] = build_program()
    res = run_bass_kernel_spmd(_CACHE["nc"], in_maps, core_ids=list(range(NCORES)))
    R = res.results
    cat = lambda k: np.concatenate([np.asarray(r[k], dtype=np.float32) for r in R], axis=0)
    return (cat("y_p"), cat("y_s"),
            cat("C_p")[None], cat("n_p")[None], cat("m_p")[None], cat("conv_p")[None],
            cat("k_p").reshape(16, 512, 8, 64)[None], cat("v_p").reshape(16, 512, 8, 64)[None],
            cat("C_s")[None], cat("n_s")[None], cat("m_s")[None], cat("conv_s")[None],
            cat("k_s").reshape(16, TS, 8, 64)[None], cat("v_s").reshape(16, TS, 8, 64)[None])
```
